# Optimizing a Trainium2 kernel written in Bass

```python
import math
import jax, jax.numpy as jnp
from jax import lax
import numpy as np

D_MODEL = 4096
BATCH = 2
SEQ = 4096
DEPTH = 1
DEC_BATCH = 8
DEC_SEQ = 2048
PAST_LEN = 128

GRID_W = 64
MIX_WIDTH = D_MODEL
ATTN_WIDTH = MIX_WIDTH // 2
FOURIER_WIDTH = MIX_WIDTH - ATTN_WIDTH
HEAD_DIM = 128
N_HEADS = ATTN_WIDTH // HEAD_DIM
N_KV_HEADS = N_HEADS // 4
KV_WIDTH = N_KV_HEADS * HEAD_DIM
N_FOURIER_GROUPS = 8
FOURIER_GROUP_DIM = FOURIER_WIDTH // N_FOURIER_GROUPS
ROPE_THETA = 10000.0
ROPE_AXIS_DIM = HEAD_DIM // 2
Q_BLOCK = 128
RMS_EPS = 1e-6
LN_EPS = 1e-5
DEEPNORM_ALPHA = (2.0 * DEPTH) ** 0.25
DEEPNORM_BETA = (8.0 * DEPTH) ** -0.25
IN_WIDTH = ATTN_WIDTH + 2 * KV_WIDTH + FOURIER_WIDTH + MIX_WIDTH
SPLITS = (ATTN_WIDTH, ATTN_WIDTH + KV_WIDTH, ATTN_WIDTH + 2 * KV_WIDTH,
          ATTN_WIDTH + 2 * KV_WIDTH + FOURIER_WIDTH)

kernel_name = "hymba_fnet_axial_gqa_encoder"


def _layernorm(x):
    xf = x.astype(jnp.float32)
    mu = jnp.mean(xf, axis=-1, keepdims=True)
    var = jnp.mean(jnp.square(xf - mu), axis=-1, keepdims=True)
    return ((xf - mu) * lax.rsqrt(var + LN_EPS)).astype(x.dtype)


def _rmsnorm(x, g):
    xf = x.astype(jnp.float32)
    y = xf * lax.rsqrt(jnp.mean(xf * xf, axis=-1, keepdims=True) + RMS_EPS)
    return (y * g.astype(jnp.float32)).astype(x.dtype)


def _axial_rope_tables(n_tokens):
    rows = n_tokens // GRID_W
    row_id = jnp.repeat(jnp.arange(rows, dtype=jnp.float32), GRID_W)
    col_id = jnp.tile(jnp.arange(GRID_W, dtype=jnp.float32), rows)
    inv_freq = ROPE_THETA ** (-jnp.arange(0, ROPE_AXIS_DIM, 2, dtype=jnp.float32) / ROPE_AXIS_DIM)
    ang_r = row_id[:, None] * inv_freq[None, :]
    ang_c = col_id[:, None] * inv_freq[None, :]
    return jnp.cos(ang_r), jnp.sin(ang_r), jnp.cos(ang_c), jnp.sin(ang_c)


def _rope_rotate(x, cos, sin):
    x1, x2 = jnp.split(x, 2, axis=-1)
    cos = cos[:, None, :].astype(x.dtype)
    sin = sin[:, None, :].astype(x.dtype)
    return jnp.concatenate([x1 * cos - x2 * sin, x1 * sin + x2 * cos], axis=-1)


def _apply_axial_rope(x, tables):
    cos_r, sin_r, cos_c, sin_c = tables
    x_row, x_col = jnp.split(x, 2, axis=-1)
    return jnp.concatenate([_rope_rotate(x_row, cos_r, sin_r), _rope_rotate(x_col, cos_c, sin_c)], axis=-1)


def _block_attention(q, k, v):
    b, s, h, d = q.shape
    n_blk = s // Q_BLOCK
    g = h // N_KV_HEADS
    qb = q.reshape(b, n_blk, Q_BLOCK, N_KV_HEADS, g, d).transpose(1, 0, 2, 3, 4, 5)
    scale = 1.0 / math.sqrt(d)

    def one_block(q_blk):
        scores = jnp.einsum('bqkgd,bskd->bkgqs', q_blk, k, preferred_element_type=jnp.float32) * scale
        p = jax.nn.softmax(scores, axis=-1).astype(v.dtype)
        return jnp.einsum('bkgqs,bskd->bqkgd', p, v)

    out = lax.map(one_block, qb)
    return out.transpose(1, 0, 2, 3, 4, 5).reshape(b, s, h * d)


def _fourier_mix(u, w_four):
    b, s, _ = u.shape
    ug = u.reshape(b, s, N_FOURIER_GROUPS, FOURIER_GROUP_DIM).astype(jnp.float32)
    f = jnp.real(jnp.fft.fft2(ug, axes=(1, 3), norm='ortho')).astype(u.dtype)
    y = jnp.einsum('bsgc,gcd->bsgd', f, w_four)
    return y.reshape(b, s, FOURIER_WIDTH)


def _encoder_layer(x, c, w_ada, b_ada, w_in, q_gain, k_gain, w_four, w_out, b_out, ln_g, ln_b):
    b, s, _ = x.shape
    mod = jnp.einsum('bd,de->be', jax.nn.silu(c), w_ada) + b_ada
    shift, scale, gate = jnp.split(mod, 3, axis=-1)
    h = _layernorm(x) * (1.0 + scale[:, None, :]) + shift[:, None, :]
    proj = jnp.einsum('bsd,de->bse', h, w_in)
    q, k, v, u_f, z = jnp.split(proj, SPLITS, axis=-1)
    q = q.reshape(b, s, N_HEADS, HEAD_DIM)
    k = k.reshape(b, s, N_KV_HEADS, HEAD_DIM)
    v = v.reshape(b, s, N_KV_HEADS, HEAD_DIM)
    tables = _axial_rope_tables(s)
    q = _apply_axial_rope(_rmsnorm(q, q_gain), tables)
    k = _apply_axial_rope(_rmsnorm(k, k_gain), tables)
    o_attn = _block_attention(q, k, v)
    o_four = _fourier_mix(u_f, w_four)
    o = jnp.concatenate([o_attn, o_four], axis=-1) * jax.nn.silu(z)
    y = jnp.einsum('bse,ed->bsd', o, w_out) + b_out
    res = DEEPNORM_ALPHA * x + gate[:, None, :] * y
    return _layernorm(res) * ln_g + ln_b


def setup_inputs(seed: int = 0) -> dict:
    key = jax.random.key(seed)
    ks = jax.random.split(key, 14)
    f32 = jnp.float32
    x_prompt = jax.random.normal(ks[0], (BATCH, SEQ, D_MODEL), f32)
    x_sample = jax.random.normal(ks[1], (DEC_BATCH, DEC_SEQ, D_MODEL), f32)
    c_prompt = jax.random.normal(ks[2], (BATCH, D_MODEL), f32)
    c_sample = jax.random.normal(ks[3], (DEC_BATCH, D_MODEL), f32)
    w_ada = jax.random.normal(ks[4], (DEPTH, D_MODEL, 3 * D_MODEL), f32) * (0.5 * D_MODEL ** -0.5)
    b_ada = jax.random.normal(ks[5], (DEPTH, 3 * D_MODEL), f32) * 0.02
    w_in = jax.random.normal(ks[6], (DEPTH, D_MODEL, IN_WIDTH), f32) * D_MODEL ** -0.5
    q_gain = 1.0 + 0.02 * jax.random.normal(ks[7], (DEPTH, HEAD_DIM), f32)
    k_gain = 1.0 + 0.02 * jax.random.normal(ks[8], (DEPTH, HEAD_DIM), f32)
    w_four = jax.random.normal(ks[9], (DEPTH, N_FOURIER_GROUPS, FOURIER_GROUP_DIM, FOURIER_GROUP_DIM), f32) * FOURIER_GROUP_DIM ** -0.5
    w_out = jax.random.normal(ks[10], (DEPTH, MIX_WIDTH, D_MODEL), f32) * (MIX_WIDTH ** -0.5 * DEEPNORM_BETA)
    b_out = jax.random.normal(ks[11], (DEPTH, D_MODEL), f32) * 0.02
    ln_g = 1.0 + 0.02 * jax.random.normal(ks[12], (DEPTH, D_MODEL), f32)
    ln_b = 0.02 * jax.random.normal(ks[13], (DEPTH, D_MODEL), f32)
    return {"x_prompt": x_prompt, "x_sample": x_sample, "c_prompt": c_prompt, "c_sample": c_sample,
            "w_ada": w_ada, "b_ada": b_ada, "w_in": w_in, "q_gain": q_gain, "k_gain": k_gain,
            "w_four": w_four, "w_out": w_out, "b_out": b_out, "ln_g": ln_g, "ln_b": ln_b}


def reference(x_prompt, x_sample, c_prompt, c_sample, w_ada, b_ada, w_in, q_gain, k_gain,
              w_four, w_out, b_out, ln_g, ln_b):
    y_prompt = x_prompt
    y_sample = x_sample
    for i in range(DEPTH):
        y_prompt = _encoder_layer(y_prompt, c_prompt, w_ada[i], b_ada[i], w_in[i], q_gain[i], k_gain[i],
                                  w_four[i], w_out[i], b_out[i], ln_g[i], ln_b[i])
        y_sample = _encoder_layer(y_sample, c_sample, w_ada[i], b_ada[i], w_in[i], q_gain[i], k_gain[i],
                                  w_four[i], w_out[i], b_out[i], ln_g[i], ln_b[i])
    return (y_prompt, y_sample)
```

```python
import contextlib
import math
import numpy as np
import ml_dtypes
import concourse.bass as bass
import concourse.mybir as mybir
from concourse.bass_utils import run_bass_kernel_spmd

F32 = mybir.dt.float32
BF16 = mybir.dt.bfloat16
AF = mybir.ActivationFunctionType
ALU = mybir.AluOpType
AX = mybir.AxisListType
NPBF = ml_dtypes.bfloat16

ENGS = ("pe", "act", "dve", "pool", "sp")
EPOOL = "dve"

RMS_EPS = 1e-6
LN_EPS = 1e-5
GRID_W = 64
ROPE_THETA = 10000.0


class Res:
    __slots__ = ("name", "ws", "rs", "prev", "ds")

    def __init__(self, name):
        self.name = name
        self.ws = []
        self.rs = []
        self.prev = []
        self.ds = None


def _compact(toks):
    best = {}
    for s, v in toks:
        if best.get(id(s), (None, -1))[1] < v:
            best[id(s)] = (s, v)
    return list(best.values())


class Prog:
    def __init__(self, nc, stack):
        self.nc = nc
        self.stack = stack
        self.ops = {e: [] for e in ENGS}
        self.esem = {e: stack.enter_context(nc.semaphore("es_" + e)) for e in ENGS if e != "sp"}
        self.ecnt = {e: 0 for e in ENGS}
        self.waited = {e: {} for e in ENGS}
        self.sems = {}
        for e, s in self.esem.items():
            self.sems[id(s)] = [s, 0]
        self.free_ds = []
        self.n_inst = 0

    def new_ds(self, name="ds"):
        if self.free_ds:
            return self.free_ds.pop()
        s = self.stack.enter_context(self.nc.semaphore(name + str(len(self.sems))))
        d = [s, 0]
        self.sems[id(s)] = d
        return d

    def release(self, resources):
        for r in resources:
            if r.ds is not None:
                self.free_ds.append(r.ds)
                r.ds = None

    def _wait(self, eng, tok):
        sem, val = tok
        w = self.waited[eng]
        if w.get(id(sem), 0) >= val:
            return
        w[id(sem)] = val
        self.ops[eng].append(lambda E, sem=sem, val=val: E.wait_ge(sem, val))
        self.n_inst += 1

    def newgen(self, r):
        r.prev = _compact(r.ws + r.rs)
        r.ws = []
        r.rs = []

    def _deps(self, eng, reads, writes, pw):
        toks = []
        for r in reads:
            toks += r.ws
        for r in writes:
            toks += r.ws
            toks += r.rs
        for r in pw:
            toks += r.prev
        for t in _compact(toks):
            self._wait(eng, t)

    def _record(self, tok, reads, writes, pw):
        for r in reads:
            r.rs.append(tok)
            if len(r.rs) > 32:
                r.rs = _compact(r.rs)
        for r in writes:
            r.ws = [tok]
            r.rs = []
            r.prev = []
        for r in pw:
            r.ws.append(tok)
            if len(r.ws) > 32:
                r.ws = _compact(r.ws)

    def op(self, eng, fn, reads=(), writes=(), pw=()):
        self._deps(eng, reads, writes, pw)
        self.ecnt[eng] += 1
        val = self.ecnt[eng]
        sem = self.esem[eng]
        self.sems[id(sem)][1] = val
        if eng == "pe":
            self.waited[eng][id(sem)] = val
        self.ops[eng].append(lambda E, fn=fn, sem=sem: fn(E).then_inc(sem, 1))
        self.n_inst += 1
        tok = (sem, val)
        self._record(tok, reads, writes, pw)
        return tok

    def dma(self, eng, out, in_, reads=(), writes=(), pw=(), ds=None, **kw):
        self._deps(eng, reads, writes, pw)
        if ds is None:
            for r in list(writes) + list(pw) + list(reads):
                if r.ds is None:
                    r.ds = self.new_ds()
                ds = r.ds
                break
        ds[1] += 16
        sem, val = ds[0], ds[1]
        self.ops[eng].append(
            lambda E, out=out, in_=in_, sem=sem, kw=kw: E.dma_start(out=out, in_=in_, **kw).then_inc(sem, 16))
        self.n_inst += 1
        tok = (sem, val)
        self._record(tok, reads, writes, pw)
        return tok

    def barrier(self, engs=ENGS):
        for e in engs:
            for sem, cnt in self.sems.values():
                if cnt > 0:
                    self._wait(e, (sem, cnt))

    def emit(self):
        nc = self.nc
        ops = self.ops
        self.ops = {e: [] for e in ENGS}
        with nc.Block() as block:
            @block.tensor
            def _(E):
                for f in ops["pe"]:
                    f(E)

            @block.scalar
            def _(E):
                for f in ops["act"]:
                    f(E)

            @block.vector
            def _(E):
                for f in ops["dve"]:
                    f(E)

            @block.gpsimd
            def _(E):
                for f in ops["pool"]:
                    f(E)

            @block.sync
            def _(E):
                for f in ops["sp"]:
                    f(E)


_UID = [0]


def _uniq(name):
    _UID[0] += 1
    return "%s_%d" % (name, _UID[0])


class Ring:
    def __init__(self, P, stack, name, n, shape, dtype, psum=False, split=1):
        nc = P.nc
        name = _uniq(name)
        self.P = P
        self.bufs = []
        for i in range(n):
            if psum:
                shp = [shape[0], shape[1] * split]
                t = stack.enter_context(nc.psum_tensor(f"{name}{i}", shp, dtype))
                if split > 1:
                    for k in range(split):
                        self.bufs.append((t[:, k * shape[1]:(k + 1) * shape[1]], Res(f"{name}{i}_{k}")))
                    continue
            else:
                t = stack.enter_context(nc.sbuf_tensor(f"{name}{i}", shape, dtype))
            self.bufs.append((t, Res(f"{name}{i}")))
        self.i = 0

    def next(self):
        b = self.bufs[self.i % len(self.bufs)]
        self.i += 1
        return b

    def release(self):
        self.P.release([r for _, r in self.bufs])


def sb(P, stack, name, shape, dtype):
    name = _uniq(name)
    t = stack.enter_context(P.nc.sbuf_tensor(name, shape, dtype))
    return t, Res(name)


def make_cfg(D=4096, SS=2048, SP=4096, T=512, NBS=512, NBP=256, T4=512, debug=False):
    c = dict(D=D, SS=SS, SP=SP, T=T, NBS=NBS, NBP=NBP, T4=T4, debug=debug)
    c["KC"] = D // 128
    c["AW"] = D // 2
    c["NH"] = c["AW"] // 128
    c["NKV"] = c["NH"] // 4
    c["KVW"] = c["NKV"] * 128
    c["FW"] = D // 2
    c["NG"] = c["FW"] // 256
    c["Z"] = D
    c["ZC"] = D // 128
    c["IN"] = c["AW"] + 2 * c["KVW"] + c["FW"] + c["Z"]
    c["SQP"] = SP // 4
    c["oq"] = 0
    c["ok"] = c["AW"]
    c["ov"] = c["AW"] + c["KVW"]
    c["ou"] = c["AW"] + 2 * c["KVW"]
    c["oz"] = c["ou"] + c["FW"]
    c["alpha"] = 2.0 ** 0.25
    return c


def build(cfg):
    nc = bass.Bass("TRN2", target_bir_lowering=False)
    D, KC, IN, ZC, NH, NKV, KVW, FW, NG, AW = (cfg[k] for k in ("D", "KC", "IN", "ZC", "NH", "NKV", "KVW", "FW", "NG", "AW"))
    SS, SP, SQP, T = cfg["SS"], cfg["SP"], cfg["SQP"], cfg["T"]
    dbg = cfg["debug"]
    phases = cfg.get("phases", (0, 1, 2, 3, 4))

    def din(name, shape, dt=F32):
        return nc.dram_tensor(name, shape, dt, kind="ExternalInput").ap()

    def dout(name, shape, dt=F32):
        return nc.dram_tensor(name, shape, dt, kind="ExternalOutput").ap()

    def dscr(name, shape, dt=BF16):
        if dbg:
            return nc.dram_tensor(name, shape, dt, kind="ExternalOutput").ap()
        return nc.dram_tensor(name, shape, dt).ap()

    w_ada = din("w_ada", [D, 3 * D])
    b_ada = din("b_ada", [1, 3 * D])
    w_in = din("w_in", [D, IN])
    w_out = din("w_out", [D, D])
    w_four = din("w_four", [NG, 256, 256])
    b_out = din("b_out", [1, D])
    ln_g = din("ln_g", [1, D])
    ln_b = din("ln_b", [1, D])
    gains = din("gains", [4, 256])
    cdft = din("cdft", [2, 256, 256])
    w_in_bf = dscr("w_in_bf", [IN // 256, 128, KC, 256])
    w_out_bf = dscr("w_out_bf", [D // 256, 128, ZC, 256])
    modr = dscr("modr", [2, 4, D], F32)

    jobs = []
    for ji, (n, Skv, Sq, NB) in enumerate((("s", SS, SS, cfg["NBS"]), ("p", SP, SQP, cfg["NBP"]))):
        j = dict(n=n, ji=ji, Skv=Skv, Sq=Sq, NB=NB)
        j["x"] = din("x" + n, [Skv, D])
        j["c"] = din("c" + n, [128, KC])
        j["cos2"] = din("cos2" + n, [Skv, 256])
        j["sin2"] = din("sin2" + n, [Skv, 256])
        j["cst"] = din("cst" + n, [Sq // NB, 128, Skv // 128, NB], BF16)
        j["sst"] = din("sst" + n, [Sq // NB, 128, Skv // 128, NB], BF16)
        j["y"] = dout("y" + n, [Sq, D])
        j["qT"] = dscr("qT" + n, [Sq // 128, 128, NH, 128])
        j["kT"] = dscr("kT" + n, [NKV, 128, Skv])
        j["v"] = dscr("v" + n, [Skv, KVW])
        j["u"] = dscr("u" + n, [Skv, FW])
        j["zsT"] = dscr("zsT" + n, [Sq // 128, 128, ZC, 128])
        j["oT"] = dscr("oT" + n, [Sq // 128, 128, ZC, 128])
        jobs.append(j)

    with contextlib.ExitStack() as gst:
        P = Prog(nc, gst)
        idf, idfr = sb(P, gst, "idf", [128, 128], F32)
        idb, idbr = sb(P, gst, "idb", [128, 128], BF16)
        onesb, onesr = sb(P, gst, "onesb", [128, 128], BF16)
        negh, neghr = sb(P, gst, "negh", [128, 2], F32)
        P.op("pool", lambda E: E.memset(idf[:], 0.0), writes=[idfr])
        P.op("pool", lambda E: E.affine_select(out=idf[:], in_=idf[:], pattern=[[-1, 128]], compare_op=ALU.not_equal,
                                               fill=1.0, base=0, channel_multiplier=1), reads=[idfr], writes=[idfr])
        P.op("dve", lambda E: E.tensor_copy(out=idb[:], in_=idf[:]), reads=[idfr], writes=[idbr])
        P.op("pool", lambda E: E.memset(onesb[:], 1.0), writes=[onesr])
        P.op("pool", lambda E: E.memset(negh[:], -0.5), writes=[neghr])
        G = dict(idb=idb, idbr=idbr, onesb=onesb, onesr=onesr, negh=negh, neghr=neghr)

        if 0 in phases:
            phase0(P, nc, cfg, jobs, dict(w_ada=w_ada, b_ada=b_ada, w_in=w_in, w_out=w_out, b_out=b_out,
                                          w_in_bf=w_in_bf, w_out_bf=w_out_bf, modr=modr), G)
            P.barrier()
        if 1 in phases:
            for j in jobs:
                phase1(P, nc, cfg, j, dict(w_in_bf=w_in_bf, modr=modr, gains=gains), G)
            P.barrier()
        if 2 in phases:
            for j in jobs:
                phase2(P, nc, cfg, j, G)
            P.barrier()
        if 3 in phases:
            phase3(P, nc, cfg, jobs, dict(w_four=w_four, cdft=cdft), G)
            P.barrier()
        if 4 in phases:
            for j in jobs:
                phase4(P, nc, cfg, j, dict(w_out_bf=w_out_bf, modr=modr, ln_g=ln_g, ln_b=ln_b), G)
            P.barrier()
        P.emit()
    return nc


def phase0(P, nc, cfg, jobs, W, G):
    D, KC, IN, ZC = cfg["D"], cfg["KC"], cfg["IN"], cfg["ZC"]
    KB = 8
    NCT = 3 * D // 512
    with contextlib.ExitStack() as st:
        lhs = []
        for j in jobs:
            ct_, cr = sb(P, st, "csb" + j["n"], [128, KC], F32)
            sc, scr = sb(P, st, "sc" + j["n"], [128, KC], F32)
            lb, lbr = sb(P, st, "lb" + j["n"], [128, KC, 128], BF16)
            P.dma("sp", ct_[:], j["c"], writes=[cr])
            P.op("act", lambda E, sc=sc, ct_=ct_: E.activation(out=sc[:], in_=ct_[:], func=AF.Silu), reads=[cr], writes=[scr])
            P.op("dve", lambda E, lb=lb, sc=sc: E.tensor_copy(out=lb[:], in_=sc[:, :, None].to_broadcast([128, KC, 128])),
                 reads=[scr], writes=[lbr])
            lhs.append((lb, lbr))
        wring = Ring(P, st, "wada", 3, [128, KB, 512], BF16)
        bring = Ring(P, st, "bada", 2, [128, 512], F32)
        boring = Ring(P, st, "bout", 2, [128, 512], F32)
        pring = Ring(P, st, "pmod", 4, [128, 512], F32, psum=True)
        ering = Ring(P, st, "emod", 4, [128, 512], F32)
        wv = W["w_ada"].rearrange("(kc p) n -> p kc n", p=128)
        cds = P.new_ds()
        wiv = W["w_in"].rearrange("(kc p) (ct n) -> ct p kc n", p=128, n=256)
        wov = W["w_out"].rearrange("(kc p) (ct n) -> ct p kc n", p=128, n=256)

        def cast_in():
            for ct in range(IN // 256):
                yield P.dma("pool", W["w_in_bf"][ct], wiv[ct], ds=cds)
            for ct in range(D // 256):
                yield P.dma("pool", W["w_out_bf"][ct], wov[ct], ds=cds)
        castgen = cast_in()
        for ct in range(NCT):
            bt, br = bring.next()
            P.dma("sp", bt[:], W["b_ada"][0, ct * 512:(ct + 1) * 512].partition_broadcast(128), writes=[br])
            isgate = ct * 512 >= 2 * D
            isscale = (ct * 512 >= D) and not isgate
            if isgate:
                bo, bor = boring.next()
                c0 = ct * 512 - 2 * D
                P.dma("sp", bo[:], W["b_out"][0, c0:c0 + 512].partition_broadcast(128), writes=[bor])
            pss = [pring.next() for _ in jobs]
            for kb in range(KC // KB):
                wt, wr = wring.next()
                P.dma("pool", wt[:], wv[:, kb * KB:(kb + 1) * KB, ct * 512:(ct + 1) * 512], writes=[wr])
                next(castgen, None)
                for k in range(KB):
                    kc = kb * KB + k
                    for ji in range(len(jobs)):
                        lb, lbr = lhs[ji]
                        pt, pr = pss[ji]
                        P.op("pe", lambda E, pt=pt, lb=lb, wt=wt, kc=kc, k=k: E.matmul(
                            pt[:], lhsT=lb[:, kc, :], rhs=wt[:, k, :], start=(kc == 0), stop=(kc == KC - 1)),
                            reads=[lbr, wr], writes=[pr])
            for ji in range(len(jobs)):
                pt, pr = pss[ji]
                et, er = ering.next()
                P.op("dve", lambda E, et=et, pt=pt, bt=bt: E.tensor_tensor(out=et[:], in0=pt[:], in1=bt[:], op=ALU.add),
                     reads=[pr, br], writes=[er])
                if isscale:
                    P.op("dve", lambda E, et=et: E.tensor_scalar(out=et[:], in0=et[:], scalar1=1.0, scalar2=None, op0=ALU.add),
                         reads=[er], writes=[er])
                row = ct * 512 // D
                c0 = ct * 512 - row * D
                P.dma("sp", W["modr"][ji, row:row + 1, c0:c0 + 512], et[0:1, :], reads=[er])
                if isgate:
                    et2, er2 = ering.next()
                    P.op("dve", lambda E, et=et, et2=et2, bo=bo: E.tensor_tensor(out=et2[:], in0=et[:], in1=bo[:], op=ALU.mult),
                         reads=[er, bor], writes=[er2])
                    P.dma("sp", W["modr"][ji, 3:4, c0:c0 + 512], et2[0:1, :], reads=[er2])
        for _ in castgen:
            pass
        P.barrier()
        P.emit()
        for r in (wring, bring, boring, ering):
            r.release()
        P.free_ds.append(cds)


def phase1(P, nc, cfg, job, W, G):
    D, KC, IN, ZC, NH, NKV, KVW, FW = (cfg[k] for k in ("D", "KC", "IN", "ZC", "NH", "NKV", "KVW", "FW"))
    T = cfg["T"]
    Skv, Sq, ji = job["Skv"], job["Sq"], job["ji"]
    NT = T // 128
    idb, idbr = G["idb"], G["idbr"]
    negh, neghr = G["negh"], G["neghr"]
    with contextlib.ExitStack() as st:
        m1, m1r = sb(P, st, "m1", [128, D], F32)
        m2, m2r = sb(P, st, "m2", [128, D], F32)
        P.dma("sp", m2[:], W["modr"][ji, 0, :].partition_broadcast(128), writes=[m2r])
        P.dma("sp", m1[:], W["modr"][ji, 1, :].partition_broadcast(128), writes=[m1r])
        gn, gnr = sb(P, st, "gn", [128, 4, 256], F32)
        P.newgen(gnr)
        for i in range(4):
            P.dma("sp", gn[:, i, :], W["gains"][i, :].partition_broadcast(128), pw=[gnr])
        xring = Ring(P, st, "x", 2, [128, D], F32)
        hring = Ring(P, st, "h", 1, [128, D], BF16)
        hT, hTr = sb(P, st, "hT", [128, KC, T], BF16)
        wring = Ring(P, st, "w", 3, [128, KC, 256], BF16)
        strg = Ring(P, st, "bst", 2, [128, (D // 512) * 6], F32)
        smr = Ring(P, st, "sm", 2, [128, 8], F32)
        ropr = Ring(P, st, "rope", 2, [128, 2, 256], F32)
        abr = Ring(P, st, "ab", NT // 2, [128, 2, 4, 256], F32)
        smq = Ring(P, st, "smq", 2, [128, 16], F32)
        sqr_ = Ring(P, st, "sq", 2, [128, 512], F32)
        xqr = Ring(P, st, "xq", 2, [128, 512], F32)
        t1r = Ring(P, st, "t1", 2, [128, 512], F32)
        t2r = Ring(P, st, "t2", 2, [128, 512], F32)
        obr = Ring(P, st, "ob", 3, [128, 512], BF16)
        sgr = Ring(P, st, "sg", 3, [128, 512], BF16)
        zsg = Ring(P, st, "zsg", 2, [128, T], BF16)
        kst, kstr = sb(P, st, "kst", [128, NKV, T], BF16)
        pth = Ring(P, st, "pth", 2, [128, 1024], BF16, psum=True)
        pmm = Ring(P, st, "pmm", 3, [128, 512], F32, psum=True)
        pz = Ring(P, st, "pz", 2, [128, T], F32, psum=True)
        ptq = Ring(P, st, "ptq", 1, [128, 512], BF16, psum=True)
        posts = []

        nst = Skv // T
        abtiles = {}
        for s_ in range(nst):
            own = s_ * T < Sq
            P.newgen(hTr)
            P.newgen(kstr)
            for tt in range(NT):
                r0 = s_ * T + tt * 128
                xt, xr = xring.next()
                P.dma("sp", xt[:], job["x"][r0:r0 + 128, :], writes=[xr])
                bs, bsr = strg.next()
                P.newgen(bsr)
                for c in range(D // 512):
                    P.op("dve", lambda E, bs=bs, xt=xt, c=c: E.bn_stats(out=bs[:, c * 6:(c + 1) * 6], in_=xt[:, c * 512:(c + 1) * 512]),
                         reads=[xr], pw=[bsr])
                sm, smr_ = smr.next()
                P.op("dve", lambda E, sm=sm, bs=bs: E.bn_aggr(out=sm[:, 0:2], in_=bs[:]), reads=[bsr], writes=[smr_])
                P.op("dve", lambda E, sm=sm: E.tensor_scalar(out=sm[:, 2:3], in0=sm[:, 1:2], scalar1=LN_EPS, scalar2=None, op0=ALU.add),
                     reads=[smr_], writes=[smr_])
                P.op("act", lambda E, sm=sm: E.activation(out=sm[:, 5:6], in_=sm[:, 2:3], func=AF.Sqrt), reads=[smr_], writes=[smr_])
                P.op("dve", lambda E, sm=sm: E.reciprocal(out=sm[:, 3:4], in_=sm[:, 5:6]), reads=[smr_], writes=[smr_])
                P.op("dve", lambda E, sm=sm: E.tensor_scalar(out=sm[:, 4:5], in0=sm[:, 0:1], scalar1=sm[:, 3:4], scalar2=-1.0,
                                                            op0=ALU.mult, op1=ALU.mult), reads=[smr_], writes=[smr_])
                xn, xnr = xt, xr
                P.op("act", lambda E, xn=xn, xt=xt, sm=sm: E.activation(out=xn[:], in_=xt[:], func=AF.Identity,
                                                                       scale=sm[:, 3:4], bias=sm[:, 4:5]),
                     reads=[smr_], writes=[xnr])
                P.op(EPOOL, lambda E, xn=xn: E.tensor_tensor(out=xn[:], in0=xn[:], in1=m1[:], op=ALU.mult),
                     reads=[xnr, m1r], writes=[xnr])
                ht, hr = hring.next()
                P.op("dve", lambda E, ht=ht, xn=xn: E.tensor_tensor(out=ht[:], in0=xn[:], in1=m2[:], op=ALU.add),
                     reads=[xnr, m2r], writes=[hr])
                for kb in range(KC // 8):
                    pt, pr = pth.next()
                    for k in range(8):
                        kc = kb * 8 + k
                        P.op("pe", lambda E, pt=pt, ht=ht, kc=kc, k=k: E.transpose(out=pt[:, k * 128:(k + 1) * 128],
                                                                                in_=ht[:, kc * 128:(kc + 1) * 128], identity=idb[:]),
                             reads=[hr, idbr], writes=[pr])
                    dst = hT[:, kb * 8:(kb + 1) * 8, tt * 128:(tt + 1) * 128]
                    src = pt[:].rearrange("p (k t) -> p k t", k=8)
                    if kb % 2 == 0:
                        P.op("act", lambda E, dst=dst, src=src: E.activation(out=dst, in_=src, func=AF.Copy), reads=[pr], pw=[hTr])
                    else:
                        P.op("dve", lambda E, dst=dst, src=src: E.tensor_copy(out=dst, in_=src), reads=[pr], pw=[hTr])
                rp, rpr = ropr.next()
                P.newgen(rpr)
                P.dma("sp", rp[:, 0, :], job["cos2"][r0:r0 + 128, :], pw=[rpr])
                P.dma("sp", rp[:, 1, :], job["sin2"][r0:r0 + 128, :], pw=[rpr])
                if tt % 2 == 0:
                    ab, abr_ = abr.next()
                    abtiles[tt // 2] = (ab, abr_)
                    P.newgen(abr_)

                def mkab(ab=ab, rp=rp, j=tt % 2):
                    def f(E):
                        return E.tensor_tensor(out=ab[:, j].rearrange("p (a b) n -> p a b n", b=2),
                                               in0=rp[:, None, :, :].to_broadcast([128, 2, 2, 256]),
                                               in1=gn[:].rearrange("p (a b) n -> p a b n", b=2), op=ALU.mult)
                    return f
                P.op(EPOOL, mkab(), reads=[rpr, gnr], pw=[abr_])
            cts = []
            if own:
                cts += [("q", c) for c in range(cfg["AW"] // 256)]
            cts += [("k", c) for c in range(KVW // 256)]
            cts += [("v", c) for c in range(KVW // 256)]
            cts += [("u", c) for c in range(FW // 256)]
            if own:
                cts += [("z", c) for c in range(cfg["Z"] // 256)]
            off = dict(q=cfg["oq"], k=cfg["ok"], v=cfg["ov"], u=cfg["ou"], z=cfg["oz"])
            wtiles = {}

            def load_w(i):
                if i < len(cts) and i not in wtiles:
                    kind, c = cts[i]
                    cti = (off[kind] + c * 256) // 256
                    wt, wr = wring.next()
                    P.dma("sp", wt[:], W["w_in_bf"][cti], writes=[wr])
                    wtiles[i] = (wt, wr)
            load_w(0)
            load_w(1)
            for i, (kind, c) in enumerate(cts):
                load_w(i + 2)
                wt, wr = wtiles.pop(i)
                if kind == "z":
                    for jz in range(2):
                        zc = c * 2 + jz
                        pt, pr = pz.next()
                        for kc in range(KC):
                            P.op("pe", lambda E, pt=pt, wt=wt, kc=kc, jz=jz: E.matmul(
                                pt[:], lhsT=wt[:, kc, jz * 128:(jz + 1) * 128], rhs=hT[:, kc, :], start=(kc == 0), stop=(kc == KC - 1)),
                                reads=[wr, hTr], writes=[pr])
                        zs, zsr = zsg.next()
                        P.op("act", lambda E, zs=zs, pt=pt: E.activation(out=zs[:], in_=pt[:], func=AF.Silu), reads=[pr], writes=[zsr])
                        dst = job["zsT"][s_ * NT:(s_ + 1) * NT, :, zc, :].rearrange("a p t -> p a t")
                        P.dma("sp", dst, zs[:].rearrange("p (a t) -> p a t", a=NT), reads=[zsr])
                    continue
                for tp in range(NT // 2):
                    r0 = s_ * T + tp * 256
                    pt, pr = pmm.next()
                    for j in range(2):
                        tt = tp * 2 + j
                        for kc in range(KC):
                            P.op("pe", lambda E, pt=pt, wt=wt, kc=kc, tt=tt, j=j: E.matmul(
                                pt[:, j * 256:(j + 1) * 256], lhsT=hT[:, kc, tt * 128:(tt + 1) * 128], rhs=wt[:, kc, :],
                                start=(kc == 0), stop=(kc == KC - 1)), reads=[wr, hTr], writes=[pr])
                    while len(posts) > 1:
                        posts.pop(0)()
                    if kind in ("v", "u"):
                        sg, sgr_ = sgr.next()
                        P.op("act", lambda E, sg=sg, pt=pt: E.activation(out=sg[:], in_=pt[:], func=AF.Copy), reads=[pr], writes=[sgr_])
                        dst = job[kind][r0:r0 + 256, c * 256:(c + 1) * 256].rearrange("(j p) n -> p j n", p=128)
                        P.dma("sp", dst, sg[:].rearrange("p (j n) -> p j n", j=2), reads=[sgr_])
                        continue
                    ab, abr_ = abtiles[tp]
                    ai = 0 if kind == "q" else 2
                    sq, sqr = sqr_.next()
                    P.op("act", lambda E, sq=sq, pt=pt: E.activation(out=sq[:], in_=pt[:], func=AF.Square), reads=[pr], writes=[sqr])
                    sm, smr_ = smq.next()
                    P.op("dve", lambda E, sm=sm, sq=sq: E.tensor_reduce(out=sm[:, 0:4], in_=sq[:].rearrange("p (h d) -> p h d", h=4),
                                                                       axis=AX.X, op=ALU.add), reads=[sqr], writes=[smr_])
                    P.op("dve", lambda E, sm=sm: E.tensor_scalar(out=sm[:, 4:8], in0=sm[:, 0:4], scalar1=1.0 / 128, scalar2=RMS_EPS,
                                                                op0=ALU.mult, op1=ALU.add), reads=[smr_], writes=[smr_])
                    P.op("act", lambda E, sm=sm: E.activation(out=sm[:, 8:12], in_=sm[:, 4:8], func=AF.Sqrt), reads=[smr_], writes=[smr_])
                    P.op("dve", lambda E, sm=sm: E.reciprocal(out=sm[:, 12:16], in_=sm[:, 8:12]), reads=[smr_], writes=[smr_])
                    xq, xqr_ = xqr.next()
                    P.op("dve", lambda E, xq=xq, pt=pt, sm=sm: E.tensor_tensor(
                        out=xq[:].rearrange("p (h d) -> p h d", h=4), in0=pt[:].rearrange("p (h d) -> p h d", h=4),
                        in1=sm[:, 12:16, None].to_broadcast([128, 4, 128]), op=ALU.mult), reads=[pr, smr_], writes=[xqr_])
                    t1, t1r_ = t1r.next()
                    P.op(EPOOL, lambda E, t1=t1, xq=xq, ab=ab, ai=ai: E.tensor_tensor(
                        out=t1[:].rearrange("p (j n) -> p j n", j=2), in0=xq[:].rearrange("p (j n) -> p j n", j=2),
                        in1=ab[:, :, ai, :], op=ALU.mult), reads=[xqr_, abr_], writes=[t1r_])
                    t2, t2r_ = t2r.next()
                    xv = xq[:].rearrange("p (j g two i) -> p j g two i", j=2, two=2, i=32)
                    bv = ab[:, :, ai + 1, :].rearrange("p j (g two i) -> p j g two i", two=2, i=32)
                    tv = t2[:].rearrange("p (j g two i) -> p j g two i", j=2, two=2, i=32)
                    P.newgen(t2r_)
                    P.op("dve", lambda E, tv=tv, xv=xv, bv=bv: E.tensor_tensor(out=tv[:, :, :, 0, :], in0=xv[:, :, :, 1, :], in1=bv[:, :, :, 0, :], op=ALU.mult),
                         reads=[xqr_, abr_], pw=[t2r_])
                    P.op("dve", lambda E, tv=tv, xv=xv, bv=bv: E.tensor_tensor(out=tv[:, :, :, 1, :], in0=xv[:, :, :, 0, :], in1=bv[:, :, :, 1, :], op=ALU.mult),
                         reads=[xqr_, abr_], pw=[t2r_])
                    ob, obr_ = obr.next()
                    P.op("dve", lambda E, ob=ob, t1=t1, t2=t2: E.tensor_tensor(out=ob[:], in0=t1[:], in1=t2[:], op=ALU.add),
                         reads=[t1r_, t2r_], writes=[obr_])

                    def post(ob=ob, obr_=obr_, kind=kind, c=c, tp=tp, s_=s_):
                        ptt, ptr = ptq.next()
                        for h in range(4):
                            P.op("pe", lambda E, ptt=ptt, ob=ob, h=h: E.transpose(out=ptt[:, h * 128:(h + 1) * 128],
                                                                               in_=ob[:, h * 128:(h + 1) * 128], identity=idb[:]),
                                 reads=[obr_, idbr], writes=[ptr])
                        if kind == "q":
                            sg, sgr_ = sgr.next()
                            P.op("act", lambda E, sg=sg, ptt=ptt: E.activation(out=sg[:], in_=ptt[:], func=AF.Copy), reads=[ptr], writes=[sgr_])
                            a0 = s_ * NT + tp * 2
                            dst = job["qT"][a0:a0 + 2, :, c * 2:c * 2 + 2, :].rearrange("a p h t -> p a h t")
                            P.dma("sp", dst, sg[:].rearrange("p (a h t) -> p a h t", a=2, h=2), reads=[sgr_])
                        else:
                            dst = kst[:, c * 2:c * 2 + 2, tp * 256:(tp + 1) * 256].rearrange("p h (j t) -> p j h t", j=2)
                            P.op("act", lambda E, dst=dst, ptt=ptt: E.activation(
                                out=dst, in_=ptt[:].rearrange("p (j h t) -> p j h t", j=2, h=2), func=AF.Copy), reads=[ptr], pw=[kstr])
                            if c == KVW // 256 - 1 and tp == NT // 2 - 1:
                                P.dma("sp", job["kT"][:, :, s_ * T:(s_ + 1) * T].rearrange("h d t -> d h t"), kst[:], reads=[kstr])
                    posts.append(post)
            while posts:
                posts.pop(0)()
        P.barrier()
        P.emit()
        for r in (xring, wring, ropr, sgr, zsg):
            r.release()
        P.release([m1r, m2r, gnr, kstr])


def phase2(P, nc, cfg, job, G):
    NH, NKV, KVW = cfg["NH"], cfg["NKV"], cfg["KVW"]
    Skv, Sq = job["Skv"], job["Sq"]
    NKT, NQB = Skv // 128, Sq // 128
    onesb, onesr = G["onesb"], G["onesr"]
    scale = 1.0 / math.sqrt(128.0)
    with contextlib.ExitStack() as st:
        kT, kTr = sb(P, st, "kTs", [128, NKV, Skv], BF16)
        vs, vsr = sb(P, st, "vs", [128, NKT, KVW], BF16)
        P.dma("sp", kT[:], job["kT"].rearrange("h d t -> d h t"), writes=[kTr])
        P.dma("sp", vs[:], job["v"].rearrange("(kt p) c -> p kt c", p=128), writes=[vsr])
        qring = Ring(P, st, "qt", 2, [128, 512], BF16)
        zring = Ring(P, st, "zt", 2, [128, 512], BF16)
        pring = Ring(P, st, "pt", 3, [128, 512], BF16)
        recr = Ring(P, st, "rec", 2, [128, 512], F32)
        onr = Ring(P, st, "on", 2, [128, 512], F32)
        ogr = Ring(P, st, "og", 2, [128, 512], BF16)
        psr = Ring(P, st, "pss", 3, [128, 512], F32, psum=True)
        por = Ring(P, st, "pso", 2, [128, 512], F32, psum=True)
        prr = Ring(P, st, "psr", 2, [128, 512], F32, psum=True)
        for g in range(NKV):
            for qb in range(NQB):
                qt, qr = qring.next()
                P.dma("sp", qt[:].rearrange("p (h t) -> p h t", h=4), job["qT"][qb][:, 4 * g:4 * g + 4, :], writes=[qr])
                zt, zr = zring.next()
                P.dma("sp", zt[:].rearrange("p (h t) -> p h t", h=4), job["zsT"][qb][:, 4 * g:4 * g + 4, :], writes=[zr])
                po, por_ = por.next()
                pr_, prr_ = prr.next()
                pend = None

                def smm(kt):
                    ps, psr_ = psr.next()
                    P.op("pe", lambda E, ps=ps, kt=kt, g=g, qt=qt: E.matmul(ps[:], lhsT=kT[:, g, kt * 128:(kt + 1) * 128], rhs=qt[:], start=True, stop=True),
                         reads=[kTr, qr], writes=[psr_])
                    pt, ptr = pring.next()
                    P.op("act", lambda E, pt=pt, ps=ps: E.activation(out=pt[:], in_=ps[:], func=AF.Exp, scale=scale), reads=[psr_], writes=[ptr])
                    return (kt, pt, ptr)

                def pvmm(item):
                    kt, pt, ptr = item
                    P.op("pe", lambda E, pt=pt, kt=kt, g=g, po=po: E.matmul(po[:], lhsT=vs[:, kt, g * 128:(g + 1) * 128], rhs=pt[:],
                                                                start=(kt == 0), stop=(kt == NKT - 1)), reads=[vsr, ptr], writes=[por_])
                    P.op("pe", lambda E, pt=pt, kt=kt, pr_=pr_: E.matmul(pr_[:], lhsT=onesb[:], rhs=pt[:],
                                                                start=(kt == 0), stop=(kt == NKT - 1)), reads=[onesr, ptr], writes=[prr_])
                items = [smm(0)]
                for kt in range(1, NKT):
                    items.append(smm(kt))
                    pvmm(items.pop(0))
                pvmm(items.pop(0))
                rec, recr_ = recr.next()
                P.op("dve", lambda E, rec=rec, pr_=pr_: E.reciprocal(out=rec[:], in_=pr_[:]), reads=[prr_], writes=[recr_])
                on, onr_ = onr.next()
                P.op("dve", lambda E, on=on, rec=rec, po=po: E.tensor_tensor(out=on[:], in0=po[:], in1=rec[:], op=ALU.mult),
                     reads=[por_, recr_], writes=[onr_])
                og, ogr_ = ogr.next()
                P.op(EPOOL, lambda E, og=og, on=on, zt=zt: E.tensor_tensor(out=og[:], in0=on[:], in1=zt[:], op=ALU.mult),
                     reads=[onr_, zr], writes=[ogr_])
                P.dma("sp", job["oT"][qb][:, 4 * g:4 * g + 4, :], og[:].rearrange("p (h t) -> p h t", h=4), reads=[ogr_])
        P.barrier()
        P.emit()
        for r in (qring, zring, ogr):
            r.release()
        P.release([kTr, vsr])


def phase3(P, nc, cfg, jobs, W, G):
    NG, NH, FW, ZC = cfg["NG"], cfg["NH"], cfg["FW"], cfg["ZC"]
    with contextlib.ExitStack() as st0:
        Ms, Msr = sb(P, st0, "Ms", [128, 2, NG, 2, 256], BF16)
        with contextlib.ExitStack() as st:
            cd, cdr = sb(P, st, "cd", [128, 2, 2, 256], F32)
            wf, wfr = sb(P, st, "wf", [128, NG, 2, 256], F32)
            P.dma("sp", cd[:], W["cdft"].rearrange("t (k p) c -> p t k c", p=128), writes=[cdr])
            P.dma("sp", wf[:], W["w_four"].rearrange("g (k p) d -> p g k d", p=128), writes=[wfr])
            pmr = Ring(P, st, "pmf", 2, [128, 256], F32, psum=True)
            for t in range(2):
                for g in range(NG):
                    for cc in range(2):
                        pt, pr = pmr.next()
                        for k in range(2):
                            P.op("pe", lambda E, pt=pt, t=t, g=g, cc=cc, k=k: E.matmul(
                                pt[:], lhsT=cd[:, t, k, cc * 128:(cc + 1) * 128], rhs=wf[:, g, k, :], start=(k == 0), stop=(k == 1)),
                                reads=[cdr, wfr], writes=[pr])
                        P.op("act", lambda E, pt=pt, t=t, g=g, cc=cc: E.activation(out=Ms[:, t, g, cc, :], in_=pt[:], func=AF.Copy),
                             reads=[pr], pw=[Msr])
            P.barrier()
            P.emit()
            P.release([cdr, wfr])
        for job in jobs:
            Skv, Sq, NB = job["Skv"], job["Sq"], job["NB"]
            NKT, NSB, NJ = Skv // 128, Sq // NB, NB // 128
            GB = min(NG, 4)
            with contextlib.ExitStack() as st:
                us, usr = sb(P, st, "us", [128, NKT, GB * 256], BF16)
                tabr = Ring(P, st, "tab", 2, [128, 2, NKT, NB], BF16)
                zfr = Ring(P, st, "zf", 2, [128, NJ, GB * 2, 128], BF16)
                ofr = Ring(P, st, "of", 2, [128, NJ, GB * 2, 128], BF16)
                abr = Ring(P, st, "abt", 2, [128, 2, 2, NB], BF16)
                pab = Ring(P, st, "pab", 4, [128, NB], F32, psum=True)
                pyr = Ring(P, st, "pyr", 2, [128, NB], F32, psum=True)
                ne = 0
                for gb in range(NG // GB):
                    P.dma("sp", us[:], job["u"][:, gb * GB * 256:(gb + 1) * GB * 256].rearrange("(kt p) c -> p kt c", p=128), writes=[usr])
                    for sbk in range(NSB):
                        tb, tbr = tabr.next()
                        P.newgen(tbr)
                        P.dma("sp", tb[:, 0], job["cst"][sbk], pw=[tbr])
                        P.dma("sp", tb[:, 1], job["sst"][sbk], pw=[tbr])
                        zc0 = NH + gb * GB * 2
                        zf, zfr_ = zfr.next()
                        P.dma("sp", zf[:], job["zsT"][sbk * NJ:(sbk + 1) * NJ, :, zc0:zc0 + GB * 2, :].rearrange("a p c t -> p a c t"), writes=[zfr_])
                        of, ofr_ = ofr.next()
                        P.newgen(ofr_)
                        for gl in range(GB):
                            g = gb * GB + gl
                            ab, abr_ = abr.next()
                            P.newgen(abr_)
                            for t in range(2):
                                for cc in range(2):
                                    pt, pr = pab.next()
                                    for kt in range(NKT):
                                        P.op("pe", lambda E, pt=pt, kt=kt, gl=gl, cc=cc, t=t, tb=tb: E.matmul(
                                            pt[:], lhsT=us[:, kt, gl * 256 + cc * 128: gl * 256 + (cc + 1) * 128], rhs=tb[:, t, kt, :],
                                            start=(kt == 0), stop=(kt == NKT - 1)), reads=[usr, tbr], writes=[pr])
                                    if ne % 2 == 0:
                                        P.op("act", lambda E, ab=ab, pt=pt, t=t, cc=cc: E.activation(out=ab[:, t, cc, :], in_=pt[:], func=AF.Copy),
                                             reads=[pr], pw=[abr_])
                                    else:
                                        P.op("dve", lambda E, ab=ab, pt=pt, t=t, cc=cc: E.tensor_copy(out=ab[:, t, cc, :], in_=pt[:]),
                                             reads=[pr], pw=[abr_])
                                    ne += 1
                            for dc in range(2):
                                py, pyr_ = pyr.next()
                                n = 0
                                for t in range(2):
                                    for cc in range(2):
                                        P.op("pe", lambda E, py=py, t=t, cc=cc, g=g, dc=dc, ab=ab, n=n: E.matmul(
                                            py[:], lhsT=Ms[:, t, g, cc, dc * 128:(dc + 1) * 128], rhs=ab[:, t, cc, :],
                                            start=(n == 0), stop=(n == 3)), reads=[Msr, abr_], writes=[pyr_])
                                        n += 1
                                ci = gl * 2 + dc
                                P.op("dve", lambda E, of=of, py=py, zf=zf, ci=ci: E.tensor_tensor(
                                    out=of[:, :, ci, :], in0=py[:].rearrange("p (a t) -> p a t", a=NJ), in1=zf[:, :, ci, :], op=ALU.mult),
                                    reads=[pyr_, zfr_], pw=[ofr_])
                        P.dma("sp", job["oT"][sbk * NJ:(sbk + 1) * NJ, :, zc0:zc0 + GB * 2, :].rearrange("a p c t -> p a c t"), of[:], reads=[ofr_])
                P.barrier()
                P.emit()
                for r in (tabr, zfr, ofr):
                    r.release()
                P.release([usr])


def phase4(P, nc, cfg, job, W, G):
    D, ZC = cfg["D"], cfg["ZC"]
    T4 = min(cfg["T4"], job["Sq"])
    NT = T4 // 128
    Sq, ji = job["Sq"], job["ji"]
    negh, neghr = G["negh"], G["neghr"]
    alpha = cfg["alpha"]
    NC = D // 256
    with contextlib.ExitStack() as st:
        lng, lngr = sb(P, st, "lng", [128, D], F32)
        lnb, lnbr = sb(P, st, "lnb", [128, D], F32)
        P.dma("sp", lng[:], W["ln_g"][0, :].partition_broadcast(128), writes=[lngr])
        P.dma("sp", lnb[:], W["ln_b"][0, :].partition_broadcast(128), writes=[lnbr])
        oT, oTr = sb(P, st, "oTs", [128, NT, ZC, 128], BF16)
        res, _ = sb(P, st, "res", [128, NT, D], F32)
        resr = [Res("res%d" % i) for i in range(NT)]
        wring = Ring(P, st, "wo", 3, [128, ZC, 256], BF16)
        gring = Ring(P, st, "gt", 3, [128, 2, 256], F32)
        tring = Ring(P, st, "tt", 3, [128, 512], F32)
        t2ring = Ring(P, st, "tt2", 3, [128, 512], F32)
        strg = Ring(P, st, "bst4", 2, [128, (D // 512) * 6], F32)
        smr = Ring(P, st, "sm4", 2, [128, 8], F32)
        pmm = Ring(P, st, "pm4", 4, [128, 512], F32, psum=True)
        for s_ in range(Sq // T4):
            P.dma("sp", oT[:], job["oT"][s_ * NT:(s_ + 1) * NT].rearrange("a p c t -> p a c t"), writes=[oTr])
            for tt in range(NT):
                r0 = s_ * T4 + tt * 128
                P.dma("sp", res[:, tt, :], job["x"][r0:r0 + 128, :], writes=[resr[tt]])
                P.newgen(resr[tt])
            wt_next = None
            for ct in range(NC):
                wt, wr = wring.next()
                P.dma("sp", wt[:], W["w_out_bf"][ct], writes=[wr])
                gt, gr = gring.next()
                P.newgen(gr)
                P.dma("sp", gt[:, 0, :], W["modr"][ji, 2, ct * 256:(ct + 1) * 256].partition_broadcast(128), pw=[gr])
                P.dma("sp", gt[:, 1, :], W["modr"][ji, 3, ct * 256:(ct + 1) * 256].partition_broadcast(128), pw=[gr])
                for tp in range(NT // 2):
                    pt, pr = pmm.next()
                    for j in range(2):
                        tt = tp * 2 + j
                        for zc in range(ZC):
                            P.op("pe", lambda E, pt=pt, wt=wt, zc=zc, tt=tt, j=j: E.matmul(
                                pt[:, j * 256:(j + 1) * 256], lhsT=oT[:, tt, zc, :], rhs=wt[:, zc, :], start=(zc == 0), stop=(zc == ZC - 1)),
                                reads=[oTr, wr], writes=[pr])
                    t1, t1r = tring.next()
                    P.op("dve", lambda E, t1=t1, pt=pt, gt=gt: E.tensor_tensor(
                        out=t1[:].rearrange("p (j n) -> p j n", j=2), in0=pt[:].rearrange("p (j n) -> p j n", j=2),
                        in1=gt[:, 0:1, :].to_broadcast([128, 2, 256]), op=ALU.mult), reads=[pr, gr], writes=[t1r])
                    t2, t2r = t2ring.next()
                    P.op(EPOOL, lambda E, t2=t2, t1=t1, gt=gt: E.tensor_tensor(
                        out=t2[:].rearrange("p (j n) -> p j n", j=2), in0=t1[:].rearrange("p (j n) -> p j n", j=2),
                        in1=gt[:, 1:2, :].to_broadcast([128, 2, 256]), op=ALU.add), reads=[t1r, gr], writes=[t2r])
                    rs = res[:, tp * 2:tp * 2 + 2, ct * 256:(ct + 1) * 256]
                    P.op("dve", lambda E, rs=rs, t2=t2: E.scalar_tensor_tensor(
                        out=rs, in0=rs, scalar=alpha, in1=t2[:].rearrange("p (j n) -> p j n", j=2), op0=ALU.mult, op1=ALU.add),
                        reads=[t2r], pw=[resr[tp * 2], resr[tp * 2 + 1]])
            for tt in range(NT):
                r0 = s_ * T4 + tt * 128
                rt = res[:, tt, :]
                rr = resr[tt]
                bs, bsr = strg.next()
                P.newgen(bsr)
                for c in range(D // 512):
                    P.op("dve", lambda E, bs=bs, rt=rt, c=c: E.bn_stats(out=bs[:, c * 6:(c + 1) * 6], in_=rt[:, c * 512:(c + 1) * 512]),
                         reads=[rr], pw=[bsr])
                sm, smr_ = smr.next()
                P.op("dve", lambda E, sm=sm, bs=bs: E.bn_aggr(out=sm[:, 0:2], in_=bs[:]), reads=[bsr], writes=[smr_])
                P.op("dve", lambda E, sm=sm: E.tensor_scalar(out=sm[:, 2:3], in0=sm[:, 1:2], scalar1=LN_EPS, scalar2=None, op0=ALU.add),
                     reads=[smr_], writes=[smr_])
                P.op("act", lambda E, sm=sm: E.activation(out=sm[:, 5:6], in_=sm[:, 2:3], func=AF.Sqrt), reads=[smr_], writes=[smr_])
                P.op("dve", lambda E, sm=sm: E.reciprocal(out=sm[:, 3:4], in_=sm[:, 5:6]), reads=[smr_], writes=[smr_])
                P.op("dve", lambda E, sm=sm: E.tensor_scalar(out=sm[:, 4:5], in0=sm[:, 0:1], scalar1=sm[:, 3:4], scalar2=-1.0,
                                                            op0=ALU.mult, op1=ALU.mult), reads=[smr_], writes=[smr_])
                P.op("act", lambda E, rt=rt, sm=sm: E.activation(out=rt, in_=rt, func=AF.Identity, scale=sm[:, 3:4], bias=sm[:, 4:5]),
                     reads=[rr, smr_], writes=[rr])
                P.op(EPOOL, lambda E, rt=rt: E.tensor_tensor(out=rt, in0=rt, in1=lng[:], op=ALU.mult), reads=[rr, lngr], writes=[rr])
                P.op("dve", lambda E, rt=rt: E.tensor_tensor(out=rt, in0=rt, in1=lnb[:], op=ALU.add), reads=[rr, lnbr], writes=[rr])
                P.dma("sp", job["y"][r0:r0 + 128, :], rt, reads=[rr])
        P.barrier()
        P.emit()
        for r in (wring, gring):
            r.release()
        P.release([lngr, lnbr, oTr] + resr)


def _rope_tables(pos):
    inv = (ROPE_THETA ** (-np.arange(0, 64, 2, dtype=np.float32) / np.float32(64))).astype(np.float32)
    row = (pos // GRID_W).astype(np.float32)
    col = (pos % GRID_W).astype(np.float32)
    ar = row[:, None] * inv[None, :]
    ac = col[:, None] * inv[None, :]
    cosr, sinr, cosc, sinc = np.cos(ar), np.sin(ar), np.cos(ac), np.sin(ac)
    cosF = np.concatenate([cosr, cosr, cosc, cosc], axis=1).astype(np.float32)
    sinS = np.concatenate([-sinr, sinr, -sinc, sinc], axis=1).astype(np.float32)
    return np.tile(cosF, (1, 2)), np.tile(sinS, (1, 2))


def _dft_tables(S, kv_pos, own_pos, NB):
    prod = (kv_pos.astype(np.int64)[:, None] * own_pos.astype(np.int64)[None, :]) % S
    ang = prod.astype(np.float64) * (2.0 * np.pi / S)
    nrm = 1.0 / math.sqrt(S)
    out = []
    for f in (np.cos, np.sin):
        m = (f(ang) * nrm).astype(np.float32).astype(NPBF)
        Skv, Sq = m.shape
        m = m.reshape(Skv // 128, 128, Sq // NB, NB).transpose(2, 1, 0, 3)
        out.append(np.ascontiguousarray(m))
    return out


def _swap_gain(g):
    g = np.asarray(g, np.float32).reshape(2, 2, 32)
    return np.ascontiguousarray(g[:, ::-1, :]).reshape(128)


def host_prep(cfg, inputs, core):
    D, KC, SS, SP, SQP = cfg["D"], cfg["KC"], cfg["SS"], cfg["SP"], cfg["SQP"]
    f32 = np.float32
    m = {}
    m["w_ada"] = np.ascontiguousarray(inputs["w_ada"][0], f32)
    m["b_ada"] = np.ascontiguousarray(inputs["b_ada"][0:1], f32)
    m["w_in"] = np.ascontiguousarray(inputs["w_in"][0], f32)
    m["w_out"] = np.ascontiguousarray(inputs["w_out"][0], f32)
    m["w_four"] = np.ascontiguousarray(inputs["w_four"][0], f32)
    m["b_out"] = np.ascontiguousarray(inputs["b_out"][0:1], f32)
    m["ln_g"] = np.ascontiguousarray(inputs["ln_g"][0:1], f32)
    m["ln_b"] = np.ascontiguousarray(inputs["ln_b"][0:1], f32)
    qg = np.asarray(inputs["q_gain"][0], f32)
    kg = np.asarray(inputs["k_gain"][0], f32)
    m["gains"] = np.stack([np.tile(qg, 2), np.tile(_swap_gain(qg), 2), np.tile(kg, 2), np.tile(_swap_gain(kg), 2)]).astype(f32)
    c = np.arange(256)
    ang = ((c[:, None] * c[None, :]) % 256).astype(np.float64) * (2 * np.pi / 256)
    m["cdft"] = np.stack([np.cos(ang) / 16.0, -np.sin(ang) / 16.0]).astype(f32)
    m["xs"] = np.ascontiguousarray(inputs["x_sample"][core], f32)
    m["cs"] = np.ascontiguousarray(np.asarray(inputs["c_sample"][core], f32).reshape(KC, 128).T)
    pos = np.arange(SS)
    m["cos2s"], m["sin2s"] = _rope_tables(pos)
    m["csts"], m["ssts"] = _dft_tables(SS, pos, pos, cfg["NBS"])
    b, q = core // 4, core % 4
    own = np.arange(q * SQP, (q + 1) * SQP)
    rest = np.concatenate([np.arange(0, q * SQP), np.arange((q + 1) * SQP, SP)])
    order = np.concatenate([own, rest]).astype(np.int64)
    m["xp"] = np.ascontiguousarray(np.asarray(inputs["x_prompt"][b], f32)[order])
    m["cp"] = np.ascontiguousarray(np.asarray(inputs["c_prompt"][b], f32).reshape(KC, 128).T)
    m["cos2p"], m["sin2p"] = _rope_tables(order)
    m["cstp"], m["sstp"] = _dft_tables(SP, order, own, cfg["NBP"])
    return m


_CACHE = {}


def kernel(**inputs):
    cfg = make_cfg()
    if "nc" not in _CACHE:
        _CACHE["nc"] = build(cfg)
    nc = _CACHE["nc"]
    n = 8
    shared = None
    in_maps = []
    for core in range(n):
        m = host_prep(cfg, inputs, core)
        if shared is None:
            shared = m
        else:
            for k in ("w_ada", "b_ada", "w_in", "w_out", "w_four", "b_out", "ln_g", "ln_b", "gains", "cdft",
                      "cos2s", "sin2s", "csts", "ssts"):
                m[k] = shared[k]
        in_maps.append(m)
    res = run_bass_kernel_spmd(nc, in_maps, core_ids=list(range(n)))
    SQP = cfg["SQP"]
    y_s = np.stack([np.asarray(res.results[i]["ys"], np.float32) for i in range(n)], axis=0)
    y_p = np.zeros((2, cfg["SP"], cfg["D"]), np.float32)
    for i in range(n):
        b, q = i // 4, i % 4
        y_p[b, q * SQP:(q + 1) * SQP] = np.asarray(res.results[i]["yp"], np.float32)
    return (y_p, y_s)
```

```python
import contextlib
import math
import numpy as np
import ml_dtypes
import concourse.bass as bass
import concourse.mybir as mybir
from concourse.bass_utils import run_bass_kernel_spmd

F32 = mybir.dt.float32
BF16 = mybir.dt.bfloat16
AF = mybir.ActivationFunctionType
ALU = mybir.AluOpType
AX = mybir.AxisListType
NPBF = ml_dtypes.bfloat16

ENGS = ("pe", "act", "dve", "pool", "sp")
EPOOL = "dve"

RMS_EPS = 1e-6
LN_EPS = 1e-5
GRID_W = 64
ROPE_THETA = 10000.0


class Res:
    __slots__ = ("name", "ws", "rs", "prev", "ds")

    def __init__(self, name):
        self.name = name
        self.ws = []
        self.rs = []
        self.prev = []
        self.ds = None


def _compact(toks):
    best = {}
    for s, v in toks:
        if best.get(id(s), (None, -1))[1] < v:
            best[id(s)] = (s, v)
    return list(best.values())


class Prog:
    def __init__(self, nc, stack):
        self.nc = nc
        self.stack = stack
        self.ops = {e: [] for e in ENGS}
        self.esem = {e: stack.enter_context(nc.semaphore("es_" + e)) for e in ENGS if e != "sp"}
        self.ecnt = {e: 0 for e in ENGS}
        self.waited = {e: {} for e in ENGS}
        self.sems = {}
        for e, s in self.esem.items():
            self.sems[id(s)] = [s, 0]
        self.free_ds = {"sw": [], "hw": []}
        self.n_inst = 0

    def new_ds(self, kind="hw"):
        if self.free_ds[kind]:
            return self.free_ds[kind].pop()
        s = self.stack.enter_context(self.nc.semaphore("ds" + kind + str(len(self.sems))))
        d = [s, 0, kind]
        self.sems[id(s)] = d
        return d

    def release(self, resources):
        for r in resources:
            if r.ds is not None:
                self.free_ds[r.ds[2]].append(r.ds)
                r.ds = None

    def _wait(self, eng, tok):
        sem, val = tok
        w = self.waited[eng]
        if w.get(id(sem), 0) >= val:
            return
        w[id(sem)] = val
        self.ops[eng].append(lambda E, sem=sem, val=val: E.wait_ge(sem, val))
        self.n_inst += 1

    def newgen(self, r):
        r.prev = _compact(r.ws + r.rs)
        r.ws = []
        r.rs = []

    def _deps(self, eng, reads, writes, pw):
        toks = []
        for r in reads:
            toks += r.ws
        for r in writes:
            toks += r.ws
            toks += r.rs
        for r in pw:
            toks += r.prev
        for t in _compact(toks):
            self._wait(eng, t)

    def _record(self, tok, reads, writes, pw):
        for r in reads:
            r.rs.append(tok)
            if len(r.rs) > 32:
                r.rs = _compact(r.rs)
        for r in writes:
            r.ws = [tok]
            r.rs = []
            r.prev = []
        for r in pw:
            r.ws.append(tok)
            if len(r.ws) > 32:
                r.ws = _compact(r.ws)

    def op(self, eng, fn, reads=(), writes=(), pw=()):
        self._deps(eng, reads, writes, pw)
        self.ecnt[eng] += 1
        val = self.ecnt[eng]
        sem = self.esem[eng]
        self.sems[id(sem)][1] = val
        if eng == "pe":
            self.waited[eng][id(sem)] = val
        self.ops[eng].append(lambda E, fn=fn, sem=sem: fn(E).then_inc(sem, 1))
        self.n_inst += 1
        tok = (sem, val)
        self._record(tok, reads, writes, pw)
        return tok

    def dma(self, eng, out, in_, reads=(), writes=(), pw=(), ds=None, **kw):
        self._deps(eng, reads, writes, pw)
        if ds is None:
            for r in list(writes) + list(pw) + list(reads):
                kind = "sw" if eng == "pool" else "hw"
                if r.ds is None:
                    r.ds = self.new_ds(kind)
                assert r.ds[2] == kind, (r.name, kind)
                ds = r.ds
                break
        ds[1] += 16
        sem, val = ds[0], ds[1]
        self.ops[eng].append(
            lambda E, out=out, in_=in_, sem=sem, kw=kw: E.dma_start(out=out, in_=in_, **kw).then_inc(sem, 16))
        self.n_inst += 1
        tok = (sem, val)
        self._record(tok, reads, writes, pw)
        return tok

    def barrier(self, engs=ENGS):
        for e in engs:
            for d in self.sems.values():
                sem, cnt = d[0], d[1]
                if cnt > 0:
                    self._wait(e, (sem, cnt))

    def emit(self):
        nc = self.nc
        ops = self.ops
        self.ops = {e: [] for e in ENGS}
        with nc.Block() as block:
            @block.tensor
            def _(E):
                for f in ops["pe"]:
                    f(E)

            @block.scalar
            def _(E):
                for f in ops["act"]:
                    f(E)

            @block.vector
            def _(E):
                for f in ops["dve"]:
                    f(E)

            @block.gpsimd
            def _(E):
                for f in ops["pool"]:
                    f(E)

            @block.sync
            def _(E):
                for f in ops["sp"]:
                    f(E)


_UID = [0]


def _uniq(name):
    _UID[0] += 1
    return "%s_%d" % (name, _UID[0])


class Ring:
    def __init__(self, P, stack, name, n, shape, dtype, psum=False, split=1):
        nc = P.nc
        name = _uniq(name)
        self.P = P
        self.bufs = []
        for i in range(n):
            if psum:
                shp = [shape[0], shape[1] * split]
                t = stack.enter_context(nc.psum_tensor(f"{name}{i}", shp, dtype))
                if split > 1:
                    for k in range(split):
                        self.bufs.append((t[:, k * shape[1]:(k + 1) * shape[1]], Res(f"{name}{i}_{k}")))
                    continue
            else:
                t = stack.enter_context(nc.sbuf_tensor(f"{name}{i}", shape, dtype))
            self.bufs.append((t, Res(f"{name}{i}")))
        self.i = 0

    def next(self):
        b = self.bufs[self.i % len(self.bufs)]
        self.i += 1
        return b

    def release(self):
        self.P.release([r for _, r in self.bufs])


def sb(P, stack, name, shape, dtype):
    name = _uniq(name)
    t = stack.enter_context(P.nc.sbuf_tensor(name, shape, dtype))
    return t, Res(name)


def make_cfg(D=4096, SS=2048, SP=4096, T=512, NBS=512, NBP=256, T4=512, debug=False):
    c = dict(D=D, SS=SS, SP=SP, T=T, NBS=NBS, NBP=NBP, T4=T4, debug=debug)
    c["KC"] = D // 128
    c["AW"] = D // 2
    c["NH"] = c["AW"] // 128
    c["NKV"] = c["NH"] // 4
    c["KVW"] = c["NKV"] * 128
    c["FW"] = D // 2
    c["NG"] = c["FW"] // 256
    c["Z"] = D
    c["ZC"] = D // 128
    c["IN"] = c["AW"] + 2 * c["KVW"] + c["FW"] + c["Z"]
    c["SQP"] = SP // 4
    c["oq"] = 0
    c["ok"] = c["AW"]
    c["ov"] = c["AW"] + c["KVW"]
    c["ou"] = c["AW"] + 2 * c["KVW"]
    c["oz"] = c["ou"] + c["FW"]
    c["alpha"] = 2.0 ** 0.25
    return c


def build(cfg):
    nc = bass.Bass("TRN2", target_bir_lowering=False)
    D, KC, IN, ZC, NH, NKV, KVW, FW, NG, AW = (cfg[k] for k in ("D", "KC", "IN", "ZC", "NH", "NKV", "KVW", "FW", "NG", "AW"))
    SS, SP, SQP, T = cfg["SS"], cfg["SP"], cfg["SQP"], cfg["T"]
    dbg = cfg["debug"]
    phases = cfg.get("phases", (0, 1, 2, 3, 4))

    def din(name, shape, dt=F32):
        return nc.dram_tensor(name, shape, dt, kind="ExternalInput").ap()

    def dout(name, shape, dt=F32):
        return nc.dram_tensor(name, shape, dt, kind="ExternalOutput").ap()

    def dscr(name, shape, dt=BF16):
        if dbg:
            return nc.dram_tensor(name, shape, dt, kind="ExternalOutput").ap()
        return nc.dram_tensor(name, shape, dt).ap()

    w_ada = din("w_ada", [D, 3 * D])
    b_ada = din("b_ada", [1, 3 * D])
    w_in = din("w_in", [D, IN])
    w_out = din("w_out", [D, D])
    w_four = din("w_four", [NG, 256, 256])
    b_out = din("b_out", [1, D])
    ln_g = din("ln_g", [1, D])
    ln_b = din("ln_b", [1, D])
    gains = din("gains", [4, 256])
    cdft = din("cdft", [2, 256, 256])
    modr = dscr("modr", [2, 4, D], F32)

    jobs = []
    for ji, (n, Skv, Sq, NB) in enumerate((("s", SS, SS, cfg["NBS"]), ("p", SP, SQP, cfg["NBP"]))):
        j = dict(n=n, ji=ji, Skv=Skv, Sq=Sq, NB=NB)
        j["x"] = din("x" + n, [Skv, D])
        j["c"] = din("c" + n, [128, KC])
        j["cos2"] = din("cos2" + n, [Skv, 256])
        j["sin2"] = din("sin2" + n, [Skv, 256])
        j["cst"] = din("cst" + n, [Sq // NB, 128, Skv // 128, NB], BF16)
        j["sst"] = din("sst" + n, [Sq // NB, 128, Skv // 128, NB], BF16)
        j["y"] = dout("y" + n, [Sq, D])
        j["qT"] = dscr("qT" + n, [Sq // 128, 128, NH, 128])
        j["kT"] = dscr("kT" + n, [NKV, 128, Skv])
        j["v"] = dscr("v" + n, [Skv, KVW])
        j["u"] = dscr("u" + n, [Skv, FW])
        j["zsT"] = dscr("zsT" + n, [Sq // 128, 128, ZC, 128])
        j["oT"] = dscr("oT" + n, [Sq // 128, 128, ZC, 128])
        jobs.append(j)

    with contextlib.ExitStack() as gst:
        P = Prog(nc, gst)
        idf, idfr = sb(P, gst, "idf", [128, 128], F32)
        idb, idbr = sb(P, gst, "idb", [128, 128], BF16)
        onesb, onesr = sb(P, gst, "onesb", [128, 128], BF16)
        negh, neghr = sb(P, gst, "negh", [128, 2], F32)
        P.op("pool", lambda E: E.memset(idf[:], 0.0), writes=[idfr])
        P.op("pool", lambda E: E.affine_select(out=idf[:], in_=idf[:], pattern=[[-1, 128]], compare_op=ALU.not_equal,
                                               fill=1.0, base=0, channel_multiplier=1), reads=[idfr], writes=[idfr])
        P.op("dve", lambda E: E.tensor_copy(out=idb[:], in_=idf[:]), reads=[idfr], writes=[idbr])
        P.op("pool", lambda E: E.memset(onesb[:], 1.0), writes=[onesr])
        P.op("pool", lambda E: E.memset(negh[:], -0.5), writes=[neghr])
        G = dict(idb=idb, idbr=idbr, onesb=onesb, onesr=onesr, negh=negh, neghr=neghr)

        if 0 in phases:
            phase0(P, nc, cfg, jobs, dict(w_ada=w_ada, b_ada=b_ada, b_out=b_out, modr=modr), G)
            P.barrier()
        if 1 in phases:
            for j in jobs:
                phase1(P, nc, cfg, j, dict(w_in_v=w_in.rearrange("(kc p) n -> p kc n", p=128), modr=modr, gains=gains), G)
            P.barrier()
        if 2 in phases:
            for j in jobs:
                phase2(P, nc, cfg, j, G)
            P.barrier()
        if 3 in phases:
            phase3(P, nc, cfg, jobs, dict(w_four=w_four, cdft=cdft), G)
            P.barrier()
        if 4 in phases:
            for j in jobs:
                phase4(P, nc, cfg, j, dict(w_out_v=w_out.rearrange("(kc p) n -> p kc n", p=128), modr=modr, ln_g=ln_g, ln_b=ln_b), G)
            P.barrier()
        P.emit()
    return nc


def phase0(P, nc, cfg, jobs, W, G):
    D, KC = cfg["D"], cfg["KC"]
    KB = 8
    NCT = 3 * D // 512
    with contextlib.ExitStack() as st:
        lb, lbr = sb(P, st, "lb", [128, KC, 128], BF16)
        P.newgen(lbr)
        for ji, j in enumerate(jobs):
            ct_, cr = sb(P, st, "csb" + j["n"], [128, KC], F32)
            sc, scr = sb(P, st, "sc" + j["n"], [128, KC], F32)
            P.dma("sp", ct_[:], j["c"], writes=[cr])
            P.op("act", lambda E, sc=sc, ct_=ct_: E.activation(out=sc[:], in_=ct_[:], func=AF.Silu), reads=[cr], writes=[scr])
            P.op("dve", lambda E, sc=sc, ji=ji: E.tensor_copy(out=lb[:, :, ji * 64:(ji + 1) * 64],
                                                             in_=sc[:, :, None].to_broadcast([128, KC, 64])),
                 reads=[scr], pw=[lbr])
        wring = Ring(P, st, "wada", 4, [128, KB, 512], BF16)
        bring = Ring(P, st, "bada", 2, [128, 512], F32)
        boring = Ring(P, st, "bout", 2, [128, 512], F32)
        pring = Ring(P, st, "pmod", 3, [128, 512], F32, psum=True)
        ering = Ring(P, st, "emod", 4, [128, 512], F32)
        wv = W["w_ada"].rearrange("(kc p) n -> p kc n", p=128)
        for ct in range(NCT):
            bt, br = bring.next()
            P.dma("sp", bt[:], W["b_ada"][0, ct * 512:(ct + 1) * 512].partition_broadcast(128), writes=[br])
            isgate = ct * 512 >= 2 * D
            isscale = (ct * 512 >= D) and not isgate
            if isgate:
                bo, bor = boring.next()
                c0 = ct * 512 - 2 * D
                P.dma("sp", bo[:], W["b_out"][0, c0:c0 + 512].partition_broadcast(128), writes=[bor])
            pt, pr = pring.next()
            for kb in range(KC // KB):
                wt, wr = wring.next()
                P.dma("pool", wt[:], wv[:, kb * KB:(kb + 1) * KB, ct * 512:(ct + 1) * 512], writes=[wr])
                for k in range(KB):
                    kc = kb * KB + k
                    P.op("pe", lambda E, pt=pt, wt=wt, kc=kc, k=k: E.matmul(
                        pt[:], lhsT=lb[:, kc, :], rhs=wt[:, k, :], start=(kc == 0), stop=(kc == KC - 1)),
                        reads=[lbr, wr], writes=[pr])
            et, er = ering.next()
            P.op("dve", lambda E, et=et, pt=pt, bt=bt: E.tensor_tensor(out=et[:], in0=pt[:], in1=bt[:], op=ALU.add),
                 reads=[pr, br], writes=[er])
            if isscale:
                P.op("dve", lambda E, et=et: E.tensor_scalar(out=et[:], in0=et[:], scalar1=1.0, scalar2=None, op0=ALU.add),
                     reads=[er], writes=[er])
            row = ct * 512 // D
            c0 = ct * 512 - row * D
            for ji in range(len(jobs)):
                P.dma("sp", W["modr"][ji, row:row + 1, c0:c0 + 512], et[ji * 64:ji * 64 + 1, :], reads=[er])
            if isgate:
                et2, er2 = ering.next()
                P.op("dve", lambda E, et=et, et2=et2, bo=bo: E.tensor_tensor(out=et2[:], in0=et[:], in1=bo[:], op=ALU.mult),
                     reads=[er, bor], writes=[er2])
                for ji in range(len(jobs)):
                    P.dma("sp", W["modr"][ji, 3:4, c0:c0 + 512], et2[ji * 64:ji * 64 + 1, :], reads=[er2])
        P.barrier()
        P.emit()
        for r in (wring, bring, boring, ering):
            r.release()


def phase1(P, nc, cfg, job, W, G):
    D, KC, IN, ZC, NH, NKV, KVW, FW = (cfg[k] for k in ("D", "KC", "IN", "ZC", "NH", "NKV", "KVW", "FW"))
    T = cfg["T"]
    Skv, Sq, ji = job["Skv"], job["Sq"], job["ji"]
    NT = T // 128
    NP = NT // 2
    idb, idbr = G["idb"], G["idbr"]
    with contextlib.ExitStack() as st:
        m1, m1r = sb(P, st, "m1", [128, D], F32)
        m2, m2r = sb(P, st, "m2", [128, D], F32)
        P.dma("sp", m2[:], W["modr"][ji, 0, :].partition_broadcast(128), writes=[m2r])
        P.dma("sp", m1[:], W["modr"][ji, 1, :].partition_broadcast(128), writes=[m1r])
        gn, gnr = sb(P, st, "gn", [128, 2, 128], F32)
        P.newgen(gnr)
        for i in range(2):
            P.dma("sp", gn[:, i, :], W["gains"][2 * i, 0:128].partition_broadcast(128), pw=[gnr])
        xring = Ring(P, st, "x", 1, [128, D], F32)
        hring = Ring(P, st, "h", 1, [128, D], BF16)
        hTs = [sb(P, st, "hT%d" % i, [128, KC, T], BF16) for i in range(2)]
        wring = Ring(P, st, "w", 2, [128, KC, 256], BF16)
        strg = Ring(P, st, "bst", 2, [128, (D // 512) * 6], F32)
        smr = Ring(P, st, "sm", 2, [128, 8], F32)
        rpr_ = Ring(P, st, "rpp", 2 * NP, [128, 2, 2, 256], F32)
        smq = Ring(P, st, "smq", 2, [128, 16], F32)
        sqr_ = Ring(P, st, "sq", 1, [128, 512], F32)
        xqr = Ring(P, st, "xq", 1, [128, 512], F32)
        t1r = Ring(P, st, "t1", 1, [128, 512], F32)
        t2r = Ring(P, st, "t2", 1, [128, 512], F32)
        obr = Ring(P, st, "ob", 3, [128, 512], BF16)
        sgr = Ring(P, st, "sg", 3, [128, 512], BF16)
        zsg = Ring(P, st, "zsg", 2, [128, T], BF16)
        kst, kstr = sb(P, st, "kst", [128, NKV, T], BF16)
        pth = Ring(P, st, "pth", 2, [128, 1024], BF16, psum=True)
        pmm = Ring(P, st, "pmm", 3, [128, 512], F32, psum=True)
        pz = Ring(P, st, "pz", 2, [128, T], F32, psum=True)
        ptq = Ring(P, st, "ptq", 1, [128, 512], BF16, psum=True)
        posts = []
        nst = Skv // T
        rptiles = {}

        def ln_gen(s_):
            hT, hTr = hTs[s_ % 2]
            P.newgen(hTr)
            for tt in range(NT):
                r0 = s_ * T + tt * 128
                if tt % 2 == 0:
                    rp, rpr = rpr_.next()
                    rptiles[(s_, tt // 2)] = (rp, rpr)
                    P.newgen(rpr)
                    for j in range(2):
                        P.dma("sp", rp[:, j, 0, :], job["cos2"][r0 + j * 128:r0 + (j + 1) * 128, :], pw=[rpr])
                        P.dma("sp", rp[:, j, 1, :], job["sin2"][r0 + j * 128:r0 + (j + 1) * 128, :], pw=[rpr])
                xt, xr = xring.next()
                P.dma("sp", xt[:], job["x"][r0:r0 + 128, :], writes=[xr])
                bs, bsr = strg.next()
                P.newgen(bsr)
                for c in range(D // 512):
                    P.op("dve", lambda E, bs=bs, xt=xt, c=c: E.bn_stats(out=bs[:, c * 6:(c + 1) * 6], in_=xt[:, c * 512:(c + 1) * 512]),
                         reads=[xr], pw=[bsr])
                sm, smr_ = smr.next()
                P.op("dve", lambda E, sm=sm, bs=bs: E.bn_aggr(out=sm[:, 0:2], in_=bs[:]), reads=[bsr], writes=[smr_])
                P.op("dve", lambda E, sm=sm: E.tensor_scalar(out=sm[:, 2:3], in0=sm[:, 1:2], scalar1=LN_EPS, scalar2=None, op0=ALU.add),
                     reads=[smr_], writes=[smr_])
                P.op("act", lambda E, sm=sm: E.activation(out=sm[:, 5:6], in_=sm[:, 2:3], func=AF.Sqrt), reads=[smr_], writes=[smr_])
                P.op("dve", lambda E, sm=sm: E.reciprocal(out=sm[:, 3:4], in_=sm[:, 5:6]), reads=[smr_], writes=[smr_])
                P.op("dve", lambda E, sm=sm: E.tensor_scalar(out=sm[:, 4:5], in0=sm[:, 0:1], scalar1=sm[:, 3:4], scalar2=-1.0,
                                                            op0=ALU.mult, op1=ALU.mult), reads=[smr_], writes=[smr_])
                P.op("act", lambda E, xt=xt, sm=sm: E.activation(out=xt[:], in_=xt[:], func=AF.Identity,
                                                                scale=sm[:, 3:4], bias=sm[:, 4:5]), reads=[smr_], writes=[xr])
                yield
                P.op(EPOOL, lambda E, xt=xt: E.tensor_tensor(out=xt[:], in0=xt[:], in1=m1[:], op=ALU.mult),
                     reads=[m1r], writes=[xr])
                ht, hr = hring.next()
                P.op("dve", lambda E, ht=ht, xt=xt: E.tensor_tensor(out=ht[:], in0=xt[:], in1=m2[:], op=ALU.add),
                     reads=[xr, m2r], writes=[hr])
                yield
                for kb in range(KC // 8):
                    pt, pr = pth.next()
                    for k in range(8):
                        kc = kb * 8 + k
                        P.op("pe", lambda E, pt=pt, ht=ht, kc=kc, k=k: E.transpose(out=pt[:, k * 128:(k + 1) * 128],
                                                                                in_=ht[:, kc * 128:(kc + 1) * 128], identity=idb[:]),
                             reads=[hr, idbr], writes=[pr])
                    dst = hT[:, kb * 8:(kb + 1) * 8, tt * 128:(tt + 1) * 128]
                    src = pt[:].rearrange("p (k t) -> p k t", k=8)
                    if kb % 2 == 0:
                        P.op("act", lambda E, dst=dst, src=src: E.activation(out=dst, in_=src, func=AF.Copy), reads=[pr], pw=[hTr])
                    else:
                        P.op("dve", lambda E, dst=dst, src=src: E.tensor_copy(out=dst, in_=src), reads=[pr], pw=[hTr])
                    yield

        for _ in ln_gen(0):
            pass
        off = dict(q=cfg["oq"], k=cfg["ok"], v=cfg["ov"], u=cfg["ou"], z=cfg["oz"])
        for s_ in range(nst):
            own = s_ * T < Sq
            hT, hTr = hTs[s_ % 2]
            P.newgen(kstr)
            nxt = ln_gen(s_ + 1) if s_ + 1 < nst else None
            cts = []
            if own:
                cts += [("q", c) for c in range(cfg["AW"] // 256)]
            cts += [("k", c) for c in range(KVW // 256)]
            cts += [("v", c) for c in range(KVW // 256)]
            cts += [("u", c) for c in range(FW // 256)]
            if own:
                cts += [("z", c) for c in range(cfg["Z"] // 256)]
            wtiles = {}

            def load_w(i):
                if i < len(cts) and i not in wtiles:
                    kind, c = cts[i]
                    col0 = off[kind] + c * 256
                    wt, wr = wring.next()
                    P.dma("pool", wt[:], W["w_in_v"][:, :, col0:col0 + 256], writes=[wr])
                    wtiles[i] = (wt, wr)
            load_w(0)
            for i, (kind, c) in enumerate(cts):
                load_w(i + 1)
                wt, wr = wtiles.pop(i)
                if kind == "z":
                    for jz in range(2):
                        zc = c * 2 + jz
                        pt, pr = pz.next()
                        for kc in range(KC):
                            P.op("pe", lambda E, pt=pt, wt=wt, kc=kc, jz=jz, hT=hT: E.matmul(
                                pt[:], lhsT=wt[:, kc, jz * 128:(jz + 1) * 128], rhs=hT[:, kc, :], start=(kc == 0), stop=(kc == KC - 1)),
                                reads=[wr, hTr], writes=[pr])
                        if nxt is not None:
                            next(nxt, None)
                        zs, zsr = zsg.next()
                        P.op("act", lambda E, zs=zs, pt=pt: E.activation(out=zs[:], in_=pt[:], func=AF.Silu), reads=[pr], writes=[zsr])
                        dst = job["zsT"][s_ * NT:(s_ + 1) * NT, :, zc, :].rearrange("a p t -> p a t")
                        P.dma("sp", dst, zs[:].rearrange("p (a t) -> p a t", a=NT), reads=[zsr])
                    continue
                for tp in range(NP):
                    r0 = s_ * T + tp * 256
                    pt, pr = pmm.next()
                    for j in range(2):
                        tt = tp * 2 + j
                        for kc in range(KC):
                            P.op("pe", lambda E, pt=pt, wt=wt, kc=kc, tt=tt, j=j, hT=hT: E.matmul(
                                pt[:, j * 256:(j + 1) * 256], lhsT=hT[:, kc, tt * 128:(tt + 1) * 128], rhs=wt[:, kc, :],
                                start=(kc == 0), stop=(kc == KC - 1)), reads=[wr, hTr], writes=[pr])
                    while len(posts) > 1:
                        posts.pop(0)()
                    if nxt is not None:
                        next(nxt, None)
                    if kind in ("v", "u"):
                        sg, sgr_ = sgr.next()
                        P.op("act", lambda E, sg=sg, pt=pt: E.activation(out=sg[:], in_=pt[:], func=AF.Copy), reads=[pr], writes=[sgr_])
                        dst = job[kind][r0:r0 + 256, c * 256:(c + 1) * 256].rearrange("(j p) n -> p j n", p=128)
                        P.dma("sp", dst, sg[:].rearrange("p (j n) -> p j n", j=2), reads=[sgr_])
                        continue
                    rp, rpr = rptiles[(s_, tp)]
                    gi = 0 if kind == "q" else 1
                    sq, sqr = sqr_.next()
                    P.op("act", lambda E, sq=sq, pt=pt: E.activation(out=sq[:], in_=pt[:], func=AF.Square), reads=[pr], writes=[sqr])
                    sm, smr_ = smq.next()
                    P.op("dve", lambda E, sm=sm, sq=sq: E.tensor_reduce(out=sm[:, 0:4], in_=sq[:].rearrange("p (h d) -> p h d", h=4),
                                                                       axis=AX.X, op=ALU.add), reads=[sqr], writes=[smr_])
                    P.op("dve", lambda E, sm=sm: E.tensor_scalar(out=sm[:, 4:8], in0=sm[:, 0:4], scalar1=1.0 / 128, scalar2=RMS_EPS,
                                                                op0=ALU.mult, op1=ALU.add), reads=[smr_], writes=[smr_])
                    P.op("act", lambda E, sm=sm: E.activation(out=sm[:, 8:12], in_=sm[:, 4:8], func=AF.Sqrt), reads=[smr_], writes=[smr_])
                    P.op("dve", lambda E, sm=sm: E.reciprocal(out=sm[:, 12:16], in_=sm[:, 8:12]), reads=[smr_], writes=[smr_])
                    xq, xqr_ = xqr.next()
                    P.newgen(xqr_)
                    for h in range(4):
                        P.op("dve", lambda E, xq=xq, pt=pt, sm=sm, h=h, gi=gi: E.scalar_tensor_tensor(
                            out=xq[:, h * 128:(h + 1) * 128], in0=pt[:, h * 128:(h + 1) * 128], scalar=sm[:, 12 + h:13 + h],
                            in1=gn[:, gi, :], op0=ALU.mult, op1=ALU.mult), reads=[pr, smr_, gnr], pw=[xqr_])
                    t1, t1r_ = t1r.next()
                    P.op(EPOOL, lambda E, t1=t1, xq=xq, rp=rp: E.tensor_tensor(
                        out=t1[:].rearrange("p (j n) -> p j n", j=2), in0=xq[:].rearrange("p (j n) -> p j n", j=2),
                        in1=rp[:, :, 0, :], op=ALU.mult), reads=[xqr_, rpr], writes=[t1r_])
                    t2, t2r_ = t2r.next()
                    xv = xq[:].rearrange("p (j g two i) -> p j g two i", j=2, two=2, i=32)
                    bv = rp[:, :, 1, :].rearrange("p j (g two i) -> p j g two i", two=2, i=32)
                    tv = t2[:].rearrange("p (j g two i) -> p j g two i", j=2, two=2, i=32)
                    P.newgen(t2r_)
                    P.op("dve", lambda E, tv=tv, xv=xv, bv=bv: E.tensor_tensor(out=tv[:, :, :, 0, :], in0=xv[:, :, :, 1, :], in1=bv[:, :, :, 0, :], op=ALU.mult),
                         reads=[xqr_, rpr], pw=[t2r_])
                    P.op("dve", lambda E, tv=tv, xv=xv, bv=bv: E.tensor_tensor(out=tv[:, :, :, 1, :], in0=xv[:, :, :, 0, :], in1=bv[:, :, :, 1, :], op=ALU.mult),
                         reads=[xqr_, rpr], pw=[t2r_])
                    ob, obr_ = obr.next()
                    P.op("dve", lambda E, ob=ob, t1=t1, t2=t2: E.tensor_tensor(out=ob[:], in0=t1[:], in1=t2[:], op=ALU.add),
                         reads=[t1r_, t2r_], writes=[obr_])

                    def post(ob=ob, obr_=obr_, kind=kind, c=c, tp=tp, s_=s_):
                        ptt, ptr = ptq.next()
                        for h in range(4):
                            P.op("pe", lambda E, ptt=ptt, ob=ob, h=h: E.transpose(out=ptt[:, h * 128:(h + 1) * 128],
                                                                               in_=ob[:, h * 128:(h + 1) * 128], identity=idb[:]),
                                 reads=[obr_, idbr], writes=[ptr])
                        if kind == "q":
                            sg, sgr_ = sgr.next()
                            P.op("act", lambda E, sg=sg, ptt=ptt: E.activation(out=sg[:], in_=ptt[:], func=AF.Copy), reads=[ptr], writes=[sgr_])
                            a0 = s_ * NT + tp * 2
                            dst = job["qT"][a0:a0 + 2, :, c * 2:c * 2 + 2, :].rearrange("a p h t -> p a h t")
                            P.dma("sp", dst, sg[:].rearrange("p (a h t) -> p a h t", a=2, h=2), reads=[sgr_])
                        else:
                            dst = kst[:, c * 2:c * 2 + 2, tp * 256:(tp + 1) * 256].rearrange("p h (j t) -> p j h t", j=2)
                            P.op("act", lambda E, dst=dst, ptt=ptt: E.activation(
                                out=dst, in_=ptt[:].rearrange("p (j h t) -> p j h t", j=2, h=2), func=AF.Copy), reads=[ptr], pw=[kstr])
                            if c == KVW // 256 - 1 and tp == NP - 1:
                                P.dma("sp", job["kT"][:, :, s_ * T:(s_ + 1) * T].rearrange("h d t -> d h t"), kst[:], reads=[kstr])
                    posts.append(post)
            while posts:
                posts.pop(0)()
            if nxt is not None:
                for _ in nxt:
                    pass
        P.barrier()
        P.emit()
        for r in (xring, wring, rpr_, sgr, zsg):
            r.release()
        P.release([m1r, m2r, gnr, kstr])


def phase2(P, nc, cfg, job, G):
    NH, NKV, KVW = cfg["NH"], cfg["NKV"], cfg["KVW"]
    Skv, Sq = job["Skv"], job["Sq"]
    NKT, NQB = Skv // 128, Sq // 128
    onesb, onesr = G["onesb"], G["onesr"]
    scale = 1.0 / math.sqrt(128.0)
    with contextlib.ExitStack() as st:
        kT, kTr = sb(P, st, "kTs", [128, NKV, Skv], BF16)
        vs, vsr = sb(P, st, "vs", [128, NKT, KVW], BF16)
        P.dma("sp", kT[:], job["kT"].rearrange("h d t -> d h t"), writes=[kTr])
        P.dma("sp", vs[:], job["v"].rearrange("(kt p) c -> p kt c", p=128), writes=[vsr])
        qring = Ring(P, st, "qt", 3, [128, 512], BF16)
        zring = Ring(P, st, "zt", 3, [128, 512], BF16)
        pring = Ring(P, st, "pt", 4, [128, 512], BF16)
        recr = Ring(P, st, "rec", 2, [128, 512], F32)
        onr = Ring(P, st, "on", 2, [128, 512], F32)
        ogr = Ring(P, st, "og", 2, [128, 512], BF16)
        psr = Ring(P, st, "pss", 4, [128, 512], F32, psum=True)
        por = Ring(P, st, "pso", 2, [128, 512], F32, psum=True)
        prr = Ring(P, st, "psr", 2, [128, 512], F32, psum=True)
        SKEW = 2
        blocks = [(g, qb) for g in range(NKV) for qb in range(NQB)]
        bstate = {}

        def begin(bi):
            g, qb = blocks[bi]
            qt, qr = qring.next()
            P.dma("sp", qt[:].rearrange("p (h t) -> p h t", h=4), job["qT"][qb][:, 4 * g:4 * g + 4, :], writes=[qr])
            zt, zr = zring.next()
            P.dma("sp", zt[:].rearrange("p (h t) -> p h t", h=4), job["zsT"][qb][:, 4 * g:4 * g + 4, :], writes=[zr])
            po, por_ = por.next()
            pr_, prr_ = prr.next()
            bstate[bi] = dict(g=g, qb=qb, qt=qt, qr=qr, zt=zt, zr=zr, po=po, por_=por_, pr_=pr_, prr_=prr_)

        def smm(bi, kt):
            b = bstate[bi]
            g, qt, qr = b["g"], b["qt"], b["qr"]
            ps, psr_ = psr.next()
            P.op("pe", lambda E, ps=ps, kt=kt, g=g, qt=qt: E.matmul(ps[:], lhsT=kT[:, g, kt * 128:(kt + 1) * 128], rhs=qt[:], start=True, stop=True),
                 reads=[kTr, qr], writes=[psr_])
            pt, ptr = pring.next()
            P.op("act", lambda E, pt=pt, ps=ps: E.activation(out=pt[:], in_=ps[:], func=AF.Exp, scale=scale), reads=[psr_], writes=[ptr])
            return (bi, kt, pt, ptr)

        def pvmm(item):
            bi, kt, pt, ptr = item
            b = bstate[bi]
            g, po, por_, pr_, prr_ = b["g"], b["po"], b["por_"], b["pr_"], b["prr_"]
            P.op("pe", lambda E, pt=pt, kt=kt, g=g, po=po: E.matmul(po[:], lhsT=vs[:, kt, g * 128:(g + 1) * 128], rhs=pt[:],
                                                                    start=(kt == 0), stop=(kt == NKT - 1)), reads=[vsr, ptr], writes=[por_])
            P.op("pe", lambda E, pt=pt, kt=kt, pr_=pr_: E.matmul(pr_[:], lhsT=onesb[:], rhs=pt[:],
                                                                 start=(kt == 0), stop=(kt == NKT - 1)), reads=[onesr, ptr], writes=[prr_])
            if kt == NKT - 1:
                finish(bi)

        def finish(bi):
            b = bstate.pop(bi)
            g, qb, zt, zr, po, por_, pr_, prr_ = (b[k] for k in ("g", "qb", "zt", "zr", "po", "por_", "pr_", "prr_"))
            rec, recr_ = recr.next()
            P.op("dve", lambda E, rec=rec, pr_=pr_: E.reciprocal(out=rec[:], in_=pr_[:]), reads=[prr_], writes=[recr_])
            on, onr_ = onr.next()
            P.op("dve", lambda E, on=on, rec=rec, po=po: E.tensor_tensor(out=on[:], in0=po[:], in1=rec[:], op=ALU.mult),
                 reads=[por_, recr_], writes=[onr_])
            og, ogr_ = ogr.next()
            P.op(EPOOL, lambda E, og=og, on=on, zt=zt: E.tensor_tensor(out=og[:], in0=on[:], in1=zt[:], op=ALU.mult),
                 reads=[onr_, zr], writes=[ogr_])
            P.dma("sp", job["oT"][qb][:, 4 * g:4 * g + 4, :], og[:].rearrange("p (h t) -> p h t", h=4), reads=[ogr_])

        tiles = [(bi, kt) for bi in range(len(blocks)) for kt in range(NKT)]
        begin(0)
        if len(blocks) > 1:
            begin(1)
        pend = []
        for i, (bi, kt) in enumerate(tiles):
            if kt == 0 and bi + 2 < len(blocks) and bi >= 0:
                pass
            pend.append(smm(bi, kt))
            if kt == 0 and bi + 1 < len(blocks) and (bi + 1) not in bstate:
                begin(bi + 1)
            if len(pend) > SKEW:
                pvmm(pend.pop(0))
        while pend:
            pvmm(pend.pop(0))
        P.barrier()
        P.emit()
        for r in (qring, zring, ogr):
            r.release()
        P.release([kTr, vsr])


def phase3(P, nc, cfg, jobs, W, G):
    NG, NH, FW, ZC = cfg["NG"], cfg["NH"], cfg["FW"], cfg["ZC"]
    with contextlib.ExitStack() as st0:
        Ms, Msr = sb(P, st0, "Ms", [128, 2, NG, 2, 256], BF16)
        with contextlib.ExitStack() as st:
            cd, cdr = sb(P, st, "cd", [128, 2, 2, 256], F32)
            wf, wfr = sb(P, st, "wf", [128, NG, 2, 256], F32)
            P.dma("sp", cd[:], W["cdft"].rearrange("t (k p) c -> p t k c", p=128), writes=[cdr])
            P.dma("sp", wf[:], W["w_four"].rearrange("g (k p) d -> p g k d", p=128), writes=[wfr])
            pmr = Ring(P, st, "pmf", 2, [128, 256], F32, psum=True)
            for t in range(2):
                for g in range(NG):
                    for cc in range(2):
                        pt, pr = pmr.next()
                        for k in range(2):
                            P.op("pe", lambda E, pt=pt, t=t, g=g, cc=cc, k=k: E.matmul(
                                pt[:], lhsT=cd[:, t, k, cc * 128:(cc + 1) * 128], rhs=wf[:, g, k, :], start=(k == 0), stop=(k == 1)),
                                reads=[cdr, wfr], writes=[pr])
                        P.op("act", lambda E, pt=pt, t=t, g=g, cc=cc: E.activation(out=Ms[:, t, g, cc, :], in_=pt[:], func=AF.Copy),
                             reads=[pr], pw=[Msr])
            P.barrier()
            P.emit()
            P.release([cdr, wfr])
        for job in jobs:
            Skv, Sq, NB = job["Skv"], job["Sq"], job["NB"]
            NKT, NSB, NJ = Skv // 128, Sq // NB, NB // 128
            GB = min(NG, 4)
            with contextlib.ExitStack() as st:
                us, usr = sb(P, st, "us", [128, NKT, GB * 256], BF16)
                tabr = Ring(P, st, "tab", 2, [128, 2, NKT, NB], BF16)
                zfr = Ring(P, st, "zf", 2, [128, NJ, GB * 2, 128], BF16)
                ofr = Ring(P, st, "of", 2, [128, NJ, GB * 2, 128], BF16)
                abr = Ring(P, st, "abt", 2, [128, 2, 2, NB], BF16)
                pab = Ring(P, st, "pab", 4, [128, NB], F32, psum=True)
                pyr = Ring(P, st, "pyr", 2, [128, NB], F32, psum=True)
                ne = 0
                for gb in range(NG // GB):
                    P.dma("sp", us[:], job["u"][:, gb * GB * 256:(gb + 1) * GB * 256].rearrange("(kt p) c -> p kt c", p=128), writes=[usr])
                    for sbk in range(NSB):
                        tb, tbr = tabr.next()
                        P.newgen(tbr)
                        P.dma("sp", tb[:, 0], job["cst"][sbk], pw=[tbr])
                        P.dma("sp", tb[:, 1], job["sst"][sbk], pw=[tbr])
                        zc0 = NH + gb * GB * 2
                        zf, zfr_ = zfr.next()
                        P.dma("sp", zf[:], job["zsT"][sbk * NJ:(sbk + 1) * NJ, :, zc0:zc0 + GB * 2, :].rearrange("a p c t -> p a c t"), writes=[zfr_])
                        of, ofr_ = ofr.next()
                        P.newgen(ofr_)
                        for gl in range(GB):
                            g = gb * GB + gl
                            ab, abr_ = abr.next()
                            P.newgen(abr_)
                            for t in range(2):
                                for cc in range(2):
                                    pt, pr = pab.next()
                                    for kt in range(NKT):
                                        P.op("pe", lambda E, pt=pt, kt=kt, gl=gl, cc=cc, t=t, tb=tb: E.matmul(
                                            pt[:], lhsT=us[:, kt, gl * 256 + cc * 128: gl * 256 + (cc + 1) * 128], rhs=tb[:, t, kt, :],
                                            start=(kt == 0), stop=(kt == NKT - 1)), reads=[usr, tbr], writes=[pr])
                                    if ne % 2 == 0:
                                        P.op("act", lambda E, ab=ab, pt=pt, t=t, cc=cc: E.activation(out=ab[:, t, cc, :], in_=pt[:], func=AF.Copy),
                                             reads=[pr], pw=[abr_])
                                    else:
                                        P.op("dve", lambda E, ab=ab, pt=pt, t=t, cc=cc: E.tensor_copy(out=ab[:, t, cc, :], in_=pt[:]),
                                             reads=[pr], pw=[abr_])
                                    ne += 1
                            for dc in range(2):
                                py, pyr_ = pyr.next()
                                n = 0
                                for t in range(2):
                                    for cc in range(2):
                                        P.op("pe", lambda E, py=py, t=t, cc=cc, g=g, dc=dc, ab=ab, n=n: E.matmul(
                                            py[:], lhsT=Ms[:, t, g, cc, dc * 128:(dc + 1) * 128], rhs=ab[:, t, cc, :],
                                            start=(n == 0), stop=(n == 3)), reads=[Msr, abr_], writes=[pyr_])
                                        n += 1
                                ci = gl * 2 + dc
                                P.op("dve", lambda E, of=of, py=py, zf=zf, ci=ci: E.tensor_tensor(
                                    out=of[:, :, ci, :], in0=py[:].rearrange("p (a t) -> p a t", a=NJ), in1=zf[:, :, ci, :], op=ALU.mult),
                                    reads=[pyr_, zfr_], pw=[ofr_])
                        P.dma("sp", job["oT"][sbk * NJ:(sbk + 1) * NJ, :, zc0:zc0 + GB * 2, :].rearrange("a p c t -> p a c t"), of[:], reads=[ofr_])
                P.barrier()
                P.emit()
                for r in (tabr, zfr, ofr):
                    r.release()
                P.release([usr])


def phase4(P, nc, cfg, job, W, G):
    D, ZC = cfg["D"], cfg["ZC"]
    T4 = min(cfg["T4"], job["Sq"])
    NT = T4 // 128
    Sq, ji = job["Sq"], job["ji"]
    negh, neghr = G["negh"], G["neghr"]
    alpha = cfg["alpha"]
    NC = D // 256
    with contextlib.ExitStack() as st:
        lng, lngr = sb(P, st, "lng", [128, D], F32)
        lnb, lnbr = sb(P, st, "lnb", [128, D], F32)
        P.dma("sp", lng[:], W["ln_g"][0, :].partition_broadcast(128), writes=[lngr])
        P.dma("sp", lnb[:], W["ln_b"][0, :].partition_broadcast(128), writes=[lnbr])
        oT, oTr = sb(P, st, "oTs", [128, NT, ZC, 128], BF16)
        res, _ = sb(P, st, "res", [128, NT, D], F32)
        resr = [Res("res%d" % i) for i in range(NT)]
        wring = Ring(P, st, "wo", 3, [128, ZC, 256], BF16)
        gring = Ring(P, st, "gt", 3, [128, 2, 256], F32)
        tring = Ring(P, st, "tt", 3, [128, 512], F32)
        t2ring = Ring(P, st, "tt2", 3, [128, 512], F32)
        strg = Ring(P, st, "bst4", 2, [128, (D // 512) * 6], F32)
        smr = Ring(P, st, "sm4", 2, [128, 8], F32)
        pmm = Ring(P, st, "pm4", 4, [128, 512], F32, psum=True)
        for s_ in range(Sq // T4):
            P.dma("sp", oT[:], job["oT"][s_ * NT:(s_ + 1) * NT].rearrange("a p c t -> p a c t"), writes=[oTr])
            for tt in range(NT):
                r0 = s_ * T4 + tt * 128
                P.dma("sp", res[:, tt, :], job["x"][r0:r0 + 128, :], writes=[resr[tt]])
                P.newgen(resr[tt])
            wt_next = None
            for ct in range(NC):
                wt, wr = wring.next()
                P.dma("pool", wt[:], W["w_out_v"][:, :, ct * 256:(ct + 1) * 256], writes=[wr])
                gt, gr = gring.next()
                P.newgen(gr)
                P.dma("sp", gt[:, 0, :], W["modr"][ji, 2, ct * 256:(ct + 1) * 256].partition_broadcast(128), pw=[gr])
                P.dma("sp", gt[:, 1, :], W["modr"][ji, 3, ct * 256:(ct + 1) * 256].partition_broadcast(128), pw=[gr])
                for tp in range(NT // 2):
                    pt, pr = pmm.next()
                    for j in range(2):
                        tt = tp * 2 + j
                        for zc in range(ZC):
                            P.op("pe", lambda E, pt=pt, wt=wt, zc=zc, tt=tt, j=j: E.matmul(
                                pt[:, j * 256:(j + 1) * 256], lhsT=oT[:, tt, zc, :], rhs=wt[:, zc, :], start=(zc == 0), stop=(zc == ZC - 1)),
                                reads=[oTr, wr], writes=[pr])
                    t1, t1r = tring.next()
                    P.op("dve", lambda E, t1=t1, pt=pt, gt=gt: E.tensor_tensor(
                        out=t1[:].rearrange("p (j n) -> p j n", j=2), in0=pt[:].rearrange("p (j n) -> p j n", j=2),
                        in1=gt[:, 0:1, :].to_broadcast([128, 2, 256]), op=ALU.mult), reads=[pr, gr], writes=[t1r])
                    t2, t2r = t2ring.next()
                    P.op(EPOOL, lambda E, t2=t2, t1=t1, gt=gt: E.tensor_tensor(
                        out=t2[:].rearrange("p (j n) -> p j n", j=2), in0=t1[:].rearrange("p (j n) -> p j n", j=2),
                        in1=gt[:, 1:2, :].to_broadcast([128, 2, 256]), op=ALU.add), reads=[t1r, gr], writes=[t2r])
                    rs = res[:, tp * 2:tp * 2 + 2, ct * 256:(ct + 1) * 256]
                    P.op("dve", lambda E, rs=rs, t2=t2: E.scalar_tensor_tensor(
                        out=rs, in0=rs, scalar=alpha, in1=t2[:].rearrange("p (j n) -> p j n", j=2), op0=ALU.mult, op1=ALU.add),
                        reads=[t2r], pw=[resr[tp * 2], resr[tp * 2 + 1]])
            for tt in range(NT):
                r0 = s_ * T4 + tt * 128
                rt = res[:, tt, :]
                rr = resr[tt]
                bs, bsr = strg.next()
                P.newgen(bsr)
                for c in range(D // 512):
                    P.op("dve", lambda E, bs=bs, rt=rt, c=c: E.bn_stats(out=bs[:, c * 6:(c + 1) * 6], in_=rt[:, c * 512:(c + 1) * 512]),
                         reads=[rr], pw=[bsr])
                sm, smr_ = smr.next()
                P.op("dve", lambda E, sm=sm, bs=bs: E.bn_aggr(out=sm[:, 0:2], in_=bs[:]), reads=[bsr], writes=[smr_])
                P.op("dve", lambda E, sm=sm: E.tensor_scalar(out=sm[:, 2:3], in0=sm[:, 1:2], scalar1=LN_EPS, scalar2=None, op0=ALU.add),
                     reads=[smr_], writes=[smr_])
                P.op("act", lambda E, sm=sm: E.activation(out=sm[:, 5:6], in_=sm[:, 2:3], func=AF.Sqrt), reads=[smr_], writes=[smr_])
                P.op("dve", lambda E, sm=sm: E.reciprocal(out=sm[:, 3:4], in_=sm[:, 5:6]), reads=[smr_], writes=[smr_])
                P.op("dve", lambda E, sm=sm: E.tensor_scalar(out=sm[:, 4:5], in0=sm[:, 0:1], scalar1=sm[:, 3:4], scalar2=-1.0,
                                                            op0=ALU.mult, op1=ALU.mult), reads=[smr_], writes=[smr_])
                P.op("act", lambda E, rt=rt, sm=sm: E.activation(out=rt, in_=rt, func=AF.Identity, scale=sm[:, 3:4], bias=sm[:, 4:5]),
                     reads=[rr, smr_], writes=[rr])
                P.op(EPOOL, lambda E, rt=rt: E.tensor_tensor(out=rt, in0=rt, in1=lng[:], op=ALU.mult), reads=[rr, lngr], writes=[rr])
                P.op("dve", lambda E, rt=rt: E.tensor_tensor(out=rt, in0=rt, in1=lnb[:], op=ALU.add), reads=[rr, lnbr], writes=[rr])
                P.dma("sp", job["y"][r0:r0 + 128, :], rt, reads=[rr])
        P.barrier()
        P.emit()
        for r in (wring, gring):
            r.release()
        P.release([lngr, lnbr, oTr] + resr)


def _rope_tables(pos):
    inv = (ROPE_THETA ** (-np.arange(0, 64, 2, dtype=np.float32) / np.float32(64))).astype(np.float32)
    row = (pos // GRID_W).astype(np.float32)
    col = (pos % GRID_W).astype(np.float32)
    ar = row[:, None] * inv[None, :]
    ac = col[:, None] * inv[None, :]
    cosr, sinr, cosc, sinc = np.cos(ar), np.sin(ar), np.cos(ac), np.sin(ac)
    cosF = np.concatenate([cosr, cosr, cosc, cosc], axis=1).astype(np.float32)
    sinS = np.concatenate([-sinr, sinr, -sinc, sinc], axis=1).astype(np.float32)
    return np.tile(cosF, (1, 2)), np.tile(sinS, (1, 2))


def _dft_tables(S, kv_pos, own_pos, NB):
    prod = (kv_pos.astype(np.int64)[:, None] * own_pos.astype(np.int64)[None, :]) % S
    ang = prod.astype(np.float64) * (2.0 * np.pi / S)
    nrm = 1.0 / math.sqrt(S)
    out = []
    for f in (np.cos, np.sin):
        m = (f(ang) * nrm).astype(np.float32).astype(NPBF)
        Skv, Sq = m.shape
        m = m.reshape(Skv // 128, 128, Sq // NB, NB).transpose(2, 1, 0, 3)
        out.append(np.ascontiguousarray(m))
    return out


def _swap_gain(g):
    g = np.asarray(g, np.float32).reshape(2, 2, 32)
    return np.ascontiguousarray(g[:, ::-1, :]).reshape(128)


def host_prep(cfg, inputs, core):
    D, KC, SS, SP, SQP = cfg["D"], cfg["KC"], cfg["SS"], cfg["SP"], cfg["SQP"]
    f32 = np.float32
    m = {}
    m["w_ada"] = np.ascontiguousarray(inputs["w_ada"][0], f32)
    m["b_ada"] = np.ascontiguousarray(inputs["b_ada"][0:1], f32)
    m["w_in"] = np.ascontiguousarray(inputs["w_in"][0], f32)
    m["w_out"] = np.ascontiguousarray(inputs["w_out"][0], f32)
    m["w_four"] = np.ascontiguousarray(inputs["w_four"][0], f32)
    m["b_out"] = np.ascontiguousarray(inputs["b_out"][0:1], f32)
    m["ln_g"] = np.ascontiguousarray(inputs["ln_g"][0:1], f32)
    m["ln_b"] = np.ascontiguousarray(inputs["ln_b"][0:1], f32)
    qg = np.asarray(inputs["q_gain"][0], f32)
    kg = np.asarray(inputs["k_gain"][0], f32)
    m["gains"] = np.stack([np.tile(qg, 2), np.tile(_swap_gain(qg), 2), np.tile(kg, 2), np.tile(_swap_gain(kg), 2)]).astype(f32)
    c = np.arange(256)
    ang = ((c[:, None] * c[None, :]) % 256).astype(np.float64) * (2 * np.pi / 256)
    m["cdft"] = np.stack([np.cos(ang) / 16.0, -np.sin(ang) / 16.0]).astype(f32)
    m["xs"] = np.ascontiguousarray(inputs["x_sample"][core], f32)
    m["cs"] = np.ascontiguousarray(np.asarray(inputs["c_sample"][core], f32).reshape(KC, 128).T)
    pos = np.arange(SS)
    m["cos2s"], m["sin2s"] = _rope_tables(pos)
    m["csts"], m["ssts"] = _dft_tables(SS, pos, pos, cfg["NBS"])
    b, q = core // 4, core % 4
    own = np.arange(q * SQP, (q + 1) * SQP)
    rest = np.concatenate([np.arange(0, q * SQP), np.arange((q + 1) * SQP, SP)])
    order = np.concatenate([own, rest]).astype(np.int64)
    m["xp"] = np.ascontiguousarray(np.asarray(inputs["x_prompt"][b], f32)[order])
    m["cp"] = np.ascontiguousarray(np.asarray(inputs["c_prompt"][b], f32).reshape(KC, 128).T)
    m["cos2p"], m["sin2p"] = _rope_tables(order)
    m["cstp"], m["sstp"] = _dft_tables(SP, order, own, cfg["NBP"])
    return m


_CACHE = {}


def kernel(**inputs):
    cfg = make_cfg()
    if "nc" not in _CACHE:
        _CACHE["nc"] = build(cfg)
    nc = _CACHE["nc"]
    n = 8
    shared = None
    in_maps = []
    for core in range(n):
        m = host_prep(cfg, inputs, core)
        if shared is None:
            shared = m
        else:
            for k in ("w_ada", "b_ada", "w_in", "w_out", "w_four", "b_out", "ln_g", "ln_b", "gains", "cdft",
                      "cos2s", "sin2s", "csts", "ssts"):
                m[k] = shared[k]
        in_maps.append(m)
    res = run_bass_kernel_spmd(nc, in_maps, core_ids=list(range(n)))
    SQP = cfg["SQP"]
    y_s = np.stack([np.asarray(res.results[i]["ys"], np.float32) for i in range(n)], axis=0)
    y_p = np.zeros((2, cfg["SP"], cfg["D"]), np.float32)
    for i in range(n):
        b, q = i // 4, i % 4
        y_p[b, q * SQP:(q + 1) * SQP] = np.asarray(res.results[i]["yp"], np.float32)
    return (y_p, y_s)
```

```python
import contextlib
import math
import numpy as np
import ml_dtypes
import concourse.bass as bass
import concourse.mybir as mybir
from concourse.bass_utils import run_bass_kernel_spmd

F32 = mybir.dt.float32
BF16 = mybir.dt.bfloat16
AF = mybir.ActivationFunctionType
ALU = mybir.AluOpType
AX = mybir.AxisListType
NPBF = ml_dtypes.bfloat16

ENGS = ("pe", "act", "dve", "pool", "sp")
EPOOL = "dve"

RMS_EPS = 1e-6
LN_EPS = 1e-5
GRID_W = 64
ROPE_THETA = 10000.0


class Res:
    __slots__ = ("name", "ws", "rs", "prev", "ds")

    def __init__(self, name):
        self.name = name
        self.ws = []
        self.rs = []
        self.prev = []
        self.ds = None


def _compact(toks):
    best = {}
    for s, v in toks:
        if best.get(id(s), (None, -1))[1] < v:
            best[id(s)] = (s, v)
    return list(best.values())


class Prog:
    def __init__(self, nc, stack):
        self.nc = nc
        self.stack = stack
        self.ops = {e: [] for e in ENGS}
        self.esem = {e: stack.enter_context(nc.semaphore("es_" + e)) for e in ENGS if e != "sp"}
        self.ecnt = {e: 0 for e in ENGS}
        self.waited = {e: {} for e in ENGS}
        self.sems = {}
        for e, s in self.esem.items():
            self.sems[id(s)] = [s, 0]
        self.free_ds = {"sw": [], "hw": []}
        self.n_inst = 0

    def new_ds(self, kind="hw"):
        if self.free_ds[kind]:
            return self.free_ds[kind].pop()
        s = self.stack.enter_context(self.nc.semaphore("ds" + kind + str(len(self.sems))))
        d = [s, 0, kind]
        self.sems[id(s)] = d
        return d

    def release(self, resources):
        for r in resources:
            if r.ds is not None:
                self.free_ds[r.ds[2]].append(r.ds)
                r.ds = None

    def _wait(self, eng, tok):
        sem, val = tok
        w = self.waited[eng]
        if w.get(id(sem), 0) >= val:
            return
        w[id(sem)] = val
        self.ops[eng].append(lambda E, sem=sem, val=val: E.wait_ge(sem, val))
        self.n_inst += 1

    def newgen(self, r):
        r.prev = _compact(r.ws + r.rs)
        r.ws = []
        r.rs = []

    def _deps(self, eng, reads, writes, pw):
        toks = []
        for r in reads:
            toks += r.ws
        for r in writes:
            toks += r.ws
            toks += r.rs
        for r in pw:
            toks += r.prev
        for t in _compact(toks):
            self._wait(eng, t)

    def _record(self, tok, reads, writes, pw):
        for r in reads:
            r.rs.append(tok)
            if len(r.rs) > 32:
                r.rs = _compact(r.rs)
        for r in writes:
            r.ws = [tok]
            r.rs = []
            r.prev = []
        for r in pw:
            r.ws.append(tok)
            if len(r.ws) > 32:
                r.ws = _compact(r.ws)

    def op(self, eng, fn, reads=(), writes=(), pw=()):
        self._deps(eng, reads, writes, pw)
        self.ecnt[eng] += 1
        val = self.ecnt[eng]
        sem = self.esem[eng]
        self.sems[id(sem)][1] = val
        if eng == "pe":
            self.waited[eng][id(sem)] = val
        self.ops[eng].append(lambda E, fn=fn, sem=sem: fn(E).then_inc(sem, 1))
        self.n_inst += 1
        tok = (sem, val)
        self._record(tok, reads, writes, pw)
        return tok

    def dma(self, eng, out, in_, reads=(), writes=(), pw=(), ds=None, **kw):
        self._deps(eng, reads, writes, pw)
        if ds is None:
            for r in list(writes) + list(pw) + list(reads):
                kind = "sw" if eng == "pool" else "hw"
                if r.ds is None:
                    r.ds = self.new_ds(kind)
                assert r.ds[2] == kind, (r.name, kind)
                ds = r.ds
                break
        ds[1] += 16
        sem, val = ds[0], ds[1]
        self.ops[eng].append(
            lambda E, out=out, in_=in_, sem=sem, kw=kw: E.dma_start(out=out, in_=in_, **kw).then_inc(sem, 16))
        self.n_inst += 1
        tok = (sem, val)
        self._record(tok, reads, writes, pw)
        return tok

    def barrier(self, engs=ENGS):
        for e in engs:
            for d in self.sems.values():
                sem, cnt = d[0], d[1]
                if cnt > 0:
                    self._wait(e, (sem, cnt))

    def emit(self):
        nc = self.nc
        ops = self.ops
        self.ops = {e: [] for e in ENGS}
        with nc.Block() as block:
            @block.tensor
            def _(E):
                for f in ops["pe"]:
                    f(E)

            @block.scalar
            def _(E):
                for f in ops["act"]:
                    f(E)

            @block.vector
            def _(E):
                for f in ops["dve"]:
                    f(E)

            @block.gpsimd
            def _(E):
                for f in ops["pool"]:
                    f(E)

            @block.sync
            def _(E):
                for f in ops["sp"]:
                    f(E)


_UID = [0]


def _uniq(name):
    _UID[0] += 1
    return "%s_%d" % (name, _UID[0])


class Ring:
    def __init__(self, P, stack, name, n, shape, dtype, psum=False, split=1):
        nc = P.nc
        name = _uniq(name)
        self.P = P
        self.bufs = []
        for i in range(n):
            if psum:
                shp = [shape[0], shape[1] * split]
                t = stack.enter_context(nc.psum_tensor(f"{name}{i}", shp, dtype))
                if split > 1:
                    for k in range(split):
                        self.bufs.append((t[:, k * shape[1]:(k + 1) * shape[1]], Res(f"{name}{i}_{k}")))
                    continue
            else:
                t = stack.enter_context(nc.sbuf_tensor(f"{name}{i}", shape, dtype))
            self.bufs.append((t, Res(f"{name}{i}")))
        self.i = 0

    def next(self):
        b = self.bufs[self.i % len(self.bufs)]
        self.i += 1
        return b

    def release(self):
        self.P.release([r for _, r in self.bufs])


def sb(P, stack, name, shape, dtype):
    name = _uniq(name)
    t = stack.enter_context(P.nc.sbuf_tensor(name, shape, dtype))
    return t, Res(name)


def make_cfg(D=4096, SS=2048, SP=4096, T=512, NBS=512, NBP=256, T4=512, debug=False):
    c = dict(D=D, SS=SS, SP=SP, T=T, NBS=NBS, NBP=NBP, T4=T4, debug=debug)
    c["KC"] = D // 128
    c["AW"] = D // 2
    c["NH"] = c["AW"] // 128
    c["NKV"] = c["NH"] // 4
    c["KVW"] = c["NKV"] * 128
    c["FW"] = D // 2
    c["NG"] = c["FW"] // 256
    c["Z"] = D
    c["ZC"] = D // 128
    c["IN"] = c["AW"] + 2 * c["KVW"] + c["FW"] + c["Z"]
    c["SQP"] = SP // 4
    c["oq"] = 0
    c["ok"] = c["AW"]
    c["ov"] = c["AW"] + c["KVW"]
    c["ou"] = c["AW"] + 2 * c["KVW"]
    c["oz"] = c["ou"] + c["FW"]
    c["alpha"] = 2.0 ** 0.25
    return c


def build(cfg):
    nc = bass.Bass("TRN2", target_bir_lowering=False)
    D, KC, IN, ZC, NH, NKV, KVW, FW, NG, AW = (cfg[k] for k in ("D", "KC", "IN", "ZC", "NH", "NKV", "KVW", "FW", "NG", "AW"))
    SS, SP, SQP, T = cfg["SS"], cfg["SP"], cfg["SQP"], cfg["T"]
    dbg = cfg["debug"]
    phases = cfg.get("phases", (0, 1, 2, 3, 4))

    def din(name, shape, dt=F32):
        return nc.dram_tensor(name, shape, dt, kind="ExternalInput").ap()

    def dout(name, shape, dt=F32):
        return nc.dram_tensor(name, shape, dt, kind="ExternalOutput").ap()

    def dscr(name, shape, dt=BF16):
        if dbg:
            return nc.dram_tensor(name, shape, dt, kind="ExternalOutput").ap()
        return nc.dram_tensor(name, shape, dt).ap()

    w_ada = din("w_ada", [D, 3 * D])
    b_ada = din("b_ada", [1, 3 * D])
    w_in = din("w_in", [D, IN])
    w_out = din("w_out", [D, D])
    w_four = din("w_four", [NG, 256, 256])
    b_out = din("b_out", [1, D])
    ln_g = din("ln_g", [1, D])
    ln_b = din("ln_b", [1, D])
    gains = din("gains", [4, 256])
    cdft = din("cdft", [2, 256, 256])
    modr = dscr("modr", [2, 4, D], F32)

    jobs = []
    for ji, (n, Skv, Sq, NB) in enumerate((("s", SS, SS, cfg["NBS"]), ("p", SP, SQP, cfg["NBP"]))):
        j = dict(n=n, ji=ji, Skv=Skv, Sq=Sq, NB=NB)
        j["x"] = din("x" + n, [Skv, D])
        j["c"] = din("c" + n, [128, KC])
        j["cos2"] = din("cos2" + n, [Skv, 256])
        j["sin2"] = din("sin2" + n, [Skv, 256])
        j["cst"] = din("cst" + n, [Sq // NB, 128, Skv // 128, NB], BF16)
        j["sst"] = din("sst" + n, [Sq // NB, 128, Skv // 128, NB], BF16)
        j["y"] = dout("y" + n, [Sq, D])
        j["qT"] = dscr("qT" + n, [Sq // 128, 128, NH, 128])
        j["kT"] = dscr("kT" + n, [NKV, 128, Skv])
        j["v"] = dscr("v" + n, [Skv, KVW])
        j["u"] = dscr("u" + n, [Skv, FW])
        j["zsT"] = dscr("zsT" + n, [Sq // 128, 128, ZC, 128])
        j["oT"] = dscr("oT" + n, [Sq // 128, 128, ZC, 128])
        jobs.append(j)

    with contextlib.ExitStack() as gst:
        P = Prog(nc, gst)
        idf, idfr = sb(P, gst, "idf", [128, 128], F32)
        idb, idbr = sb(P, gst, "idb", [128, 128], BF16)
        onesb, onesr = sb(P, gst, "onesb", [128, 128], BF16)
        negh, neghr = sb(P, gst, "negh", [128, 2], F32)
        P.op("pool", lambda E: E.memset(idf[:], 0.0), writes=[idfr])
        P.op("pool", lambda E: E.affine_select(out=idf[:], in_=idf[:], pattern=[[-1, 128]], compare_op=ALU.not_equal,
                                               fill=1.0, base=0, channel_multiplier=1), reads=[idfr], writes=[idfr])
        P.op("dve", lambda E: E.tensor_copy(out=idb[:], in_=idf[:]), reads=[idfr], writes=[idbr])
        P.op("pool", lambda E: E.memset(onesb[:], 1.0), writes=[onesr])
        P.op("pool", lambda E: E.memset(negh[:], -0.5), writes=[neghr])
        G = dict(idf=idf, idfr=idfr, idb=idb, idbr=idbr, onesb=onesb, onesr=onesr, negh=negh, neghr=neghr)

        if 0 in phases:
            phase0(P, nc, cfg, jobs, dict(w_ada=w_ada, b_ada=b_ada, b_out=b_out, modr=modr), G)
            P.barrier()
        if 1 in phases:
            for j in jobs:
                phase1(P, nc, cfg, j, dict(w_in_v=w_in.rearrange("(kc p) n -> p kc n", p=128), modr=modr, gains=gains), G)
            P.barrier()
        if 2 in phases:
            for j in jobs:
                phase2(P, nc, cfg, j, G)
            P.barrier()
        if 3 in phases:
            phase3(P, nc, cfg, jobs, dict(w_four=w_four, cdft=cdft), G)
            P.barrier()
        if 4 in phases:
            for j in jobs:
                phase4(P, nc, cfg, j, dict(w_out_v=w_out.rearrange("(kc p) n -> p kc n", p=128), modr=modr, ln_g=ln_g, ln_b=ln_b), G)
            P.barrier()
        P.emit()
    return nc


def phase0(P, nc, cfg, jobs, W, G):
    D, KC = cfg["D"], cfg["KC"]
    KB = 8
    NCT = 3 * D // 512
    with contextlib.ExitStack() as st:
        lb, lbr = sb(P, st, "lb", [128, KC, 128], BF16)
        P.newgen(lbr)
        for ji, j in enumerate(jobs):
            ct_, cr = sb(P, st, "csb" + j["n"], [128, KC], F32)
            sc, scr = sb(P, st, "sc" + j["n"], [128, KC], F32)
            P.dma("sp", ct_[:], j["c"], writes=[cr])
            P.op("act", lambda E, sc=sc, ct_=ct_: E.activation(out=sc[:], in_=ct_[:], func=AF.Silu), reads=[cr], writes=[scr])
            P.op("dve", lambda E, sc=sc, ji=ji: E.tensor_copy(out=lb[:, :, ji * 64:(ji + 1) * 64],
                                                             in_=sc[:, :, None].to_broadcast([128, KC, 64])),
                 reads=[scr], pw=[lbr])
        wring = Ring(P, st, "wada", 4, [128, KB, 512], BF16)
        bring = Ring(P, st, "bada", 2, [128, 512], F32)
        boring = Ring(P, st, "bout", 2, [128, 512], F32)
        pring = Ring(P, st, "pmod", 3, [128, 512], F32, psum=True)
        ering = Ring(P, st, "emod", 4, [128, 512], F32)
        wv = W["w_ada"].rearrange("(kc p) n -> p kc n", p=128)
        for ct in range(NCT):
            bt, br = bring.next()
            P.dma("sp", bt[:], W["b_ada"][0, ct * 512:(ct + 1) * 512].partition_broadcast(128), writes=[br])
            isgate = ct * 512 >= 2 * D
            isscale = (ct * 512 >= D) and not isgate
            if isgate:
                bo, bor = boring.next()
                c0 = ct * 512 - 2 * D
                P.dma("sp", bo[:], W["b_out"][0, c0:c0 + 512].partition_broadcast(128), writes=[bor])
            pt, pr = pring.next()
            for kb in range(KC // KB):
                wt, wr = wring.next()
                P.dma("pool", wt[:], wv[:, kb * KB:(kb + 1) * KB, ct * 512:(ct + 1) * 512], writes=[wr])
                for k in range(KB):
                    kc = kb * KB + k
                    P.op("pe", lambda E, pt=pt, wt=wt, kc=kc, k=k: E.matmul(
                        pt[:], lhsT=lb[:, kc, :], rhs=wt[:, k, :], start=(kc == 0), stop=(kc == KC - 1)),
                        reads=[lbr, wr], writes=[pr])
            et, er = ering.next()
            P.op("dve", lambda E, et=et, pt=pt, bt=bt: E.tensor_tensor(out=et[:], in0=pt[:], in1=bt[:], op=ALU.add),
                 reads=[pr, br], writes=[er])
            if isscale:
                P.op("dve", lambda E, et=et: E.tensor_scalar(out=et[:], in0=et[:], scalar1=1.0, scalar2=None, op0=ALU.add),
                     reads=[er], writes=[er])
            row = ct * 512 // D
            c0 = ct * 512 - row * D
            for ji in range(len(jobs)):
                P.dma("sp", W["modr"][ji, row:row + 1, c0:c0 + 512], et[ji * 64:ji * 64 + 1, :], reads=[er])
            if isgate:
                et2, er2 = ering.next()
                P.op("dve", lambda E, et=et, et2=et2, bo=bo: E.tensor_tensor(out=et2[:], in0=et[:], in1=bo[:], op=ALU.mult),
                     reads=[er, bor], writes=[er2])
                for ji in range(len(jobs)):
                    P.dma("sp", W["modr"][ji, 3:4, c0:c0 + 512], et2[ji * 64:ji * 64 + 1, :], reads=[er2])
        P.barrier()
        P.emit()
        for r in (wring, bring, boring, ering):
            r.release()


def phase1(P, nc, cfg, job, W, G):
    D, KC, IN, ZC, NH, NKV, KVW, FW = (cfg[k] for k in ("D", "KC", "IN", "ZC", "NH", "NKV", "KVW", "FW"))
    T = cfg["T"]
    Skv, Sq, ji = job["Skv"], job["Sq"], job["ji"]
    NT = T // 128
    CW = 512
    idb, idbr = G["idb"], G["idbr"]
    idf, idfr = G["idf"], G["idfr"]
    with contextlib.ExitStack() as st:
        mT, mTr = sb(P, st, "mT", [128, 2, KC], F32)
        P.newgen(mTr)
        import os
        if os.environ.get("KDBG") == "nomt":
            P.op("dve", lambda E: E.memset(mT[:], 1.0), pw=[mTr])
        else:
            for i, row in ((0, 1), (1, 0)):
                for kc in range(KC):
                    P.dma("sp", mT[:, i, kc:kc + 1], W["modr"][ji, row, kc * 128:(kc + 1) * 128].rearrange("(p o) -> p o", o=1), pw=[mTr])
        gn, gnr = sb(P, st, "gn", [128, 2, 128], F32)
        P.newgen(gnr)
        for i in range(2):
            P.dma("sp", gn[:, i, :], W["gains"][2 * i, 0:128].partition_broadcast(128), pw=[gnr])
        xring = Ring(P, st, "x", 2, [128, D], F32)
        hTs = [sb(P, st, "hT%d" % i, [128, KC, T], BF16) for i in range(2)]
        wring = Ring(P, st, "w", 2, [128, KC, CW], BF16)
        strg = Ring(P, st, "bst", 2, [128, (D // 512) * 6], F32)
        smr = Ring(P, st, "sm", 2, [128, 8], F32)
        rpr_ = Ring(P, st, "rpp", 2 * NT, [128, 2, 128], F32)
        smq = Ring(P, st, "smq", 2, [128, 16], F32)
        sqr_ = Ring(P, st, "sq", 1, [128, 512], F32)
        xqr = Ring(P, st, "xq", 1, [128, 512], F32)
        t1r = Ring(P, st, "t1", 1, [128, 512], F32)
        t2r = Ring(P, st, "t2", 1, [128, 512], F32)
        obr = Ring(P, st, "ob", 3, [128, 512], BF16)
        sgr = Ring(P, st, "sg", 3, [128, 512], BF16)
        zsg = Ring(P, st, "zsg", 2, [128, T], BF16)
        kst, kstr = sb(P, st, "kst", [128, NKV, T], BF16)
        pth = Ring(P, st, "pth", 2, [128, 512], F32, psum=True)
        pmm = Ring(P, st, "pmm", 3, [128, 512], F32, psum=True)
        pz = Ring(P, st, "pz", 2, [128, T], F32, psum=True)
        ptq = Ring(P, st, "ptq", 1, [128, 512], BF16, psum=True)
        posts = []
        nst = Skv // T
        rptiles = {}
        nev = [0]

        def ln_gen(s_):
            hT, hTr = hTs[s_ % 2]
            P.newgen(hTr)
            for tt in range(NT):
                r0 = s_ * T + tt * 128
                rp, rpr = rpr_.next()
                rptiles[(s_, tt)] = (rp, rpr)
                P.newgen(rpr)
                P.dma("sp", rp[:, 0, :], job["cos2"][r0:r0 + 128, 0:128], pw=[rpr])
                P.dma("sp", rp[:, 1, :], job["sin2"][r0:r0 + 128, 0:128], pw=[rpr])
                xt, xr = xring.next()
                P.dma("sp", xt[:], job["x"][r0:r0 + 128, :], writes=[xr])
                bs, bsr = strg.next()
                P.newgen(bsr)
                for c in range(D // 512):
                    P.op("dve", lambda E, bs=bs, xt=xt, c=c: E.bn_stats(out=bs[:, c * 6:(c + 1) * 6], in_=xt[:, c * 512:(c + 1) * 512]),
                         reads=[xr], pw=[bsr])
                sm, smr_ = smr.next()
                P.op("dve", lambda E, sm=sm, bs=bs: E.bn_aggr(out=sm[:, 0:2], in_=bs[:]), reads=[bsr], writes=[smr_])
                P.op("dve", lambda E, sm=sm: E.tensor_scalar(out=sm[:, 2:3], in0=sm[:, 1:2], scalar1=LN_EPS, scalar2=None, op0=ALU.add),
                     reads=[smr_], writes=[smr_])
                P.op("act", lambda E, sm=sm: E.activation(out=sm[:, 5:6], in_=sm[:, 2:3], func=AF.Sqrt), reads=[smr_], writes=[smr_])
                P.op("dve", lambda E, sm=sm: E.reciprocal(out=sm[:, 3:4], in_=sm[:, 5:6]), reads=[smr_], writes=[smr_])
                P.op("dve", lambda E, sm=sm: E.tensor_scalar(out=sm[:, 4:5], in0=sm[:, 0:1], scalar1=sm[:, 3:4], scalar2=-1.0,
                                                            op0=ALU.mult, op1=ALU.mult), reads=[smr_], writes=[smr_])
                P.op("act", lambda E, xt=xt, sm=sm: E.activation(out=xt[:], in_=xt[:], func=AF.Identity,
                                                                scale=sm[:, 3:4], bias=sm[:, 4:5]), reads=[smr_], writes=[xr])
                yield
                for kb in range(KC // 4):
                    pt, pr = pth.next()
                    for k in range(4):
                        kc = kb * 4 + k
                        P.op("pe", lambda E, pt=pt, xt=xt, kc=kc, k=k: E.transpose(out=pt[:, k * 128:(k + 1) * 128],
                                                                                in_=xt[:, kc * 128:(kc + 1) * 128], identity=idf[:]),
                             reads=[xr, idfr], writes=[pr])
                    for k in range(4):
                        kc = kb * 4 + k
                        dst = hT[:, kc, tt * 128:(tt + 1) * 128]
                        src = pt[:, k * 128:(k + 1) * 128]
                        if kb % 2 == 0 and os.environ.get("KDBG2") != "dve":
                            P.op("act", lambda E, dst=dst, src=src, kc=kc: E.activation(
                                out=dst, in_=src, func=AF.Identity, scale=mT[:, 0, kc:kc + 1], bias=mT[:, 1, kc:kc + 1]),
                                reads=[pr, mTr], pw=[hTr])
                        else:
                            P.op("dve", lambda E, dst=dst, src=src, kc=kc: E.tensor_scalar(
                                out=dst, in0=src, scalar1=mT[:, 0, kc:kc + 1], scalar2=mT[:, 1, kc:kc + 1], op0=ALU.mult, op1=ALU.add),
                                reads=[pr, mTr], pw=[hTr])
                        nev[0] += 1
                    if kb % 2 == 1:
                        yield

        for _ in ln_gen(0):
            pass
        off = dict(q=cfg["oq"], k=cfg["ok"], v=cfg["ov"], u=cfg["ou"], z=cfg["oz"])
        for s_ in range(nst):
            own = s_ * T < Sq
            hT, hTr = hTs[s_ % 2]
            P.newgen(kstr)
            nxt = ln_gen(s_ + 1) if s_ + 1 < nst else None
            cts = []
            if own:
                cts += [("q", c) for c in range(cfg["AW"] // CW)]
            cts += [("k", c) for c in range(KVW // CW)]
            cts += [("v", c) for c in range(KVW // CW)]
            cts += [("u", c) for c in range(FW // CW)]
            if own:
                cts += [("z", c) for c in range(cfg["Z"] // CW)]
            wtiles = {}

            def load_w(i):
                if i < len(cts) and i not in wtiles:
                    kind, c = cts[i]
                    col0 = off[kind] + c * CW
                    wt, wr = wring.next()
                    P.newgen(wr)
                    for hh in range(CW // 256):
                        P.dma("pool", wt[:, :, hh * 256:(hh + 1) * 256], W["w_in_v"][:, :, col0 + hh * 256:col0 + (hh + 1) * 256], pw=[wr])
                    wtiles[i] = (wt, wr)
            load_w(0)
            for i, (kind, c) in enumerate(cts):
                load_w(i + 1)
                wt, wr = wtiles.pop(i)
                if kind == "z":
                    for jz in range(CW // 128):
                        zc = c * (CW // 128) + jz
                        pt, pr = pz.next()
                        for kc in range(KC):
                            P.op("pe", lambda E, pt=pt, wt=wt, kc=kc, jz=jz, hT=hT: E.matmul(
                                pt[:], lhsT=wt[:, kc, jz * 128:(jz + 1) * 128], rhs=hT[:, kc, :], start=(kc == 0), stop=(kc == KC - 1)),
                                reads=[wr, hTr], writes=[pr])
                        if nxt is not None:
                            next(nxt, None)
                        zs, zsr = zsg.next()
                        P.op("act", lambda E, zs=zs, pt=pt: E.activation(out=zs[:], in_=pt[:], func=AF.Silu), reads=[pr], writes=[zsr])
                        dst = job["zsT"][s_ * NT:(s_ + 1) * NT, :, zc, :].rearrange("a p t -> p a t")
                        P.dma("sp", dst, zs[:].rearrange("p (a t) -> p a t", a=NT), reads=[zsr])
                    continue
                for tt in range(NT):
                    r0 = s_ * T + tt * 128
                    pt, pr = pmm.next()
                    for kc in range(KC):
                        P.op("pe", lambda E, pt=pt, wt=wt, kc=kc, tt=tt, hT=hT: E.matmul(
                            pt[:], lhsT=hT[:, kc, tt * 128:(tt + 1) * 128], rhs=wt[:, kc, :],
                            start=(kc == 0), stop=(kc == KC - 1)), reads=[wr, hTr], writes=[pr])
                    while len(posts) > 1:
                        posts.pop(0)()
                    if nxt is not None:
                        next(nxt, None)
                    if kind in ("v", "u"):
                        sg, sgr_ = sgr.next()
                        P.op("act", lambda E, sg=sg, pt=pt: E.activation(out=sg[:], in_=pt[:], func=AF.Copy), reads=[pr], writes=[sgr_])
                        P.dma("sp", job[kind][r0:r0 + 128, c * CW:(c + 1) * CW], sg[:], reads=[sgr_])
                        continue
                    rp, rpr = rptiles[(s_, tt)]
                    gi = 0 if kind == "q" else 1
                    sq, sqr = sqr_.next()
                    P.op("act", lambda E, sq=sq, pt=pt: E.activation(out=sq[:], in_=pt[:], func=AF.Square), reads=[pr], writes=[sqr])
                    sm, smr_ = smq.next()
                    P.op("dve", lambda E, sm=sm, sq=sq: E.tensor_reduce(out=sm[:, 0:4], in_=sq[:].rearrange("p (h d) -> p h d", h=4),
                                                                       axis=AX.X, op=ALU.add), reads=[sqr], writes=[smr_])
                    P.op("dve", lambda E, sm=sm: E.tensor_scalar(out=sm[:, 4:8], in0=sm[:, 0:4], scalar1=1.0 / 128, scalar2=RMS_EPS,
                                                                op0=ALU.mult, op1=ALU.add), reads=[smr_], writes=[smr_])
                    P.op("act", lambda E, sm=sm: E.activation(out=sm[:, 8:12], in_=sm[:, 4:8], func=AF.Sqrt), reads=[smr_], writes=[smr_])
                    P.op("dve", lambda E, sm=sm: E.reciprocal(out=sm[:, 12:16], in_=sm[:, 8:12]), reads=[smr_], writes=[smr_])
                    xq, xqr_ = xqr.next()
                    P.newgen(xqr_)
                    for h in range(4):
                        P.op("dve", lambda E, xq=xq, pt=pt, sm=sm, h=h, gi=gi: E.scalar_tensor_tensor(
                            out=xq[:, h * 128:(h + 1) * 128], in0=pt[:, h * 128:(h + 1) * 128], scalar=sm[:, 12 + h:13 + h],
                            in1=gn[:, gi, :], op0=ALU.mult, op1=ALU.mult), reads=[pr, smr_, gnr], pw=[xqr_])
                    t1, t1r_ = t1r.next()
                    P.op(EPOOL, lambda E, t1=t1, xq=xq, rp=rp: E.tensor_tensor(
                        out=t1[:].rearrange("p (h n) -> p h n", h=4), in0=xq[:].rearrange("p (h n) -> p h n", h=4),
                        in1=rp[:, 0:1, :].to_broadcast([128, 4, 128]), op=ALU.mult), reads=[xqr_, rpr], writes=[t1r_])
                    t2, t2r_ = t2r.next()
                    xv = xq[:].rearrange("p (h g two i) -> p h g two i", h=4, two=2, i=32)
                    tv = t2[:].rearrange("p (h g two i) -> p h g two i", h=4, two=2, i=32)
                    bv = rp[:, 1:2, :].rearrange("p o (g two i) -> p o g two i", two=2, i=32)
                    P.newgen(t2r_)
                    P.op("dve", lambda E, tv=tv, xv=xv, bv=bv: E.tensor_tensor(
                        out=tv[:, :, :, 0, :], in0=xv[:, :, :, 1, :], in1=bv[:, :, :, 0, :].to_broadcast([128, 4, 2, 32]), op=ALU.mult),
                        reads=[xqr_, rpr], pw=[t2r_])
                    P.op("dve", lambda E, tv=tv, xv=xv, bv=bv: E.tensor_tensor(
                        out=tv[:, :, :, 1, :], in0=xv[:, :, :, 0, :], in1=bv[:, :, :, 1, :].to_broadcast([128, 4, 2, 32]), op=ALU.mult),
                        reads=[xqr_, rpr], pw=[t2r_])
                    ob, obr_ = obr.next()
                    P.op("dve", lambda E, ob=ob, t1=t1, t2=t2: E.tensor_tensor(out=ob[:], in0=t1[:], in1=t2[:], op=ALU.add),
                         reads=[t1r_, t2r_], writes=[obr_])

                    def post(ob=ob, obr_=obr_, kind=kind, c=c, tt=tt, s_=s_):
                        ptt, ptr = ptq.next()
                        for h in range(4):
                            P.op("pe", lambda E, ptt=ptt, ob=ob, h=h: E.transpose(out=ptt[:, h * 128:(h + 1) * 128],
                                                                               in_=ob[:, h * 128:(h + 1) * 128], identity=idb[:]),
                                 reads=[obr_, idbr], writes=[ptr])
                        if kind == "q":
                            sg, sgr_ = sgr.next()
                            P.op("act", lambda E, sg=sg, ptt=ptt: E.activation(out=sg[:], in_=ptt[:], func=AF.Copy), reads=[ptr], writes=[sgr_])
                            dst = job["qT"][s_ * NT + tt][:, c * 4:c * 4 + 4, :]
                            P.dma("sp", dst, sg[:].rearrange("p (h t) -> p h t", h=4), reads=[sgr_])
                        else:
                            dst = kst[:, c * 4:c * 4 + 4, tt * 128:(tt + 1) * 128]
                            P.op("act", lambda E, dst=dst, ptt=ptt: E.activation(
                                out=dst, in_=ptt[:].rearrange("p (h t) -> p h t", h=4), func=AF.Copy), reads=[ptr], pw=[kstr])
                            if c == KVW // CW - 1 and tt == NT - 1:
                                P.dma("sp", job["kT"][:, :, s_ * T:(s_ + 1) * T].rearrange("h d t -> d h t"), kst[:], reads=[kstr])
                    posts.append(post)
            while posts:
                posts.pop(0)()
            if nxt is not None:
                for _ in nxt:
                    pass
        P.barrier()
        P.emit()
        for r in (xring, wring, rpr_, sgr, zsg):
            r.release()
        P.release([mTr, gnr, kstr])


def phase2(P, nc, cfg, job, G):
    NH, NKV, KVW = cfg["NH"], cfg["NKV"], cfg["KVW"]
    Skv, Sq = job["Skv"], job["Sq"]
    NKT, NQB = Skv // 128, Sq // 128
    onesb, onesr = G["onesb"], G["onesr"]
    scale = 1.0 / math.sqrt(128.0)
    with contextlib.ExitStack() as st:
        kT, kTr = sb(P, st, "kTs", [128, NKV, Skv], BF16)
        vs, vsr = sb(P, st, "vs", [128, NKT, KVW], BF16)
        P.dma("sp", kT[:], job["kT"].rearrange("h d t -> d h t"), writes=[kTr])
        P.dma("sp", vs[:], job["v"].rearrange("(kt p) c -> p kt c", p=128), writes=[vsr])
        qring = Ring(P, st, "qt", 3, [128, 512], BF16)
        zring = Ring(P, st, "zt", 3, [128, 512], BF16)
        pring = Ring(P, st, "pt", 4, [128, 512], BF16)
        recr = Ring(P, st, "rec", 2, [128, 512], F32)
        onr = Ring(P, st, "on", 2, [128, 512], F32)
        ogr = Ring(P, st, "og", 2, [128, 512], BF16)
        psr = Ring(P, st, "pss", 4, [128, 512], F32, psum=True)
        por = Ring(P, st, "pso", 2, [128, 512], F32, psum=True)
        prr = Ring(P, st, "psr", 2, [128, 512], F32, psum=True)
        SKEW = 2
        blocks = [(g, qb) for g in range(NKV) for qb in range(NQB)]
        bstate = {}

        def begin(bi):
            g, qb = blocks[bi]
            qt, qr = qring.next()
            P.dma("sp", qt[:].rearrange("p (h t) -> p h t", h=4), job["qT"][qb][:, 4 * g:4 * g + 4, :], writes=[qr])
            zt, zr = zring.next()
            P.dma("sp", zt[:].rearrange("p (h t) -> p h t", h=4), job["zsT"][qb][:, 4 * g:4 * g + 4, :], writes=[zr])
            po, por_ = por.next()
            pr_, prr_ = prr.next()
            bstate[bi] = dict(g=g, qb=qb, qt=qt, qr=qr, zt=zt, zr=zr, po=po, por_=por_, pr_=pr_, prr_=prr_)

        def smm(bi, kt):
            b = bstate[bi]
            g, qt, qr = b["g"], b["qt"], b["qr"]
            ps, psr_ = psr.next()
            P.op("pe", lambda E, ps=ps, kt=kt, g=g, qt=qt: E.matmul(ps[:], lhsT=kT[:, g, kt * 128:(kt + 1) * 128], rhs=qt[:], start=True, stop=True),
                 reads=[kTr, qr], writes=[psr_])
            pt, ptr = pring.next()
            P.op("act", lambda E, pt=pt, ps=ps: E.activation(out=pt[:], in_=ps[:], func=AF.Exp, scale=scale), reads=[psr_], writes=[ptr])
            return (bi, kt, pt, ptr)

        def pvmm(item):
            bi, kt, pt, ptr = item
            b = bstate[bi]
            g, po, por_, pr_, prr_ = b["g"], b["po"], b["por_"], b["pr_"], b["prr_"]
            P.op("pe", lambda E, pt=pt, kt=kt, g=g, po=po: E.matmul(po[:], lhsT=vs[:, kt, g * 128:(g + 1) * 128], rhs=pt[:],
                                                                    start=(kt == 0), stop=(kt == NKT - 1)), reads=[vsr, ptr], writes=[por_])
            P.op("pe", lambda E, pt=pt, kt=kt, pr_=pr_: E.matmul(pr_[:], lhsT=onesb[:], rhs=pt[:],
                                                                 start=(kt == 0), stop=(kt == NKT - 1)), reads=[onesr, ptr], writes=[prr_])
            if kt == NKT - 1:
                finish(bi)

        def finish(bi):
            b = bstate.pop(bi)
            g, qb, zt, zr, po, por_, pr_, prr_ = (b[k] for k in ("g", "qb", "zt", "zr", "po", "por_", "pr_", "prr_"))
            rec, recr_ = recr.next()
            P.op("dve", lambda E, rec=rec, pr_=pr_: E.reciprocal(out=rec[:], in_=pr_[:]), reads=[prr_], writes=[recr_])
            on, onr_ = onr.next()
            P.op("dve", lambda E, on=on, rec=rec, po=po: E.tensor_tensor(out=on[:], in0=po[:], in1=rec[:], op=ALU.mult),
                 reads=[por_, recr_], writes=[onr_])
            og, ogr_ = ogr.next()
            P.op(EPOOL, lambda E, og=og, on=on, zt=zt: E.tensor_tensor(out=og[:], in0=on[:], in1=zt[:], op=ALU.mult),
                 reads=[onr_, zr], writes=[ogr_])
            P.dma("sp", job["oT"][qb][:, 4 * g:4 * g + 4, :], og[:].rearrange("p (h t) -> p h t", h=4), reads=[ogr_])

        tiles = [(bi, kt) for bi in range(len(blocks)) for kt in range(NKT)]
        begin(0)
        if len(blocks) > 1:
            begin(1)
        pend = []
        for i, (bi, kt) in enumerate(tiles):
            if kt == 0 and bi + 2 < len(blocks) and bi >= 0:
                pass
            pend.append(smm(bi, kt))
            if kt == 0 and bi + 1 < len(blocks) and (bi + 1) not in bstate:
                begin(bi + 1)
            if len(pend) > SKEW:
                pvmm(pend.pop(0))
        while pend:
            pvmm(pend.pop(0))
        P.barrier()
        P.emit()
        for r in (qring, zring, ogr):
            r.release()
        P.release([kTr, vsr])


def phase3(P, nc, cfg, jobs, W, G):
    NG, NH, FW, ZC = cfg["NG"], cfg["NH"], cfg["FW"], cfg["ZC"]
    with contextlib.ExitStack() as st0:
        Ms, Msr = sb(P, st0, "Ms", [128, 2, NG, 2, 256], BF16)
        with contextlib.ExitStack() as st:
            cd, cdr = sb(P, st, "cd", [128, 2, 2, 256], F32)
            wf, wfr = sb(P, st, "wf", [128, NG, 2, 256], F32)
            P.dma("sp", cd[:], W["cdft"].rearrange("t (k p) c -> p t k c", p=128), writes=[cdr])
            P.dma("sp", wf[:], W["w_four"].rearrange("g (k p) d -> p g k d", p=128), writes=[wfr])
            pmr = Ring(P, st, "pmf", 2, [128, 256], F32, psum=True)
            for t in range(2):
                for g in range(NG):
                    for cc in range(2):
                        pt, pr = pmr.next()
                        for k in range(2):
                            P.op("pe", lambda E, pt=pt, t=t, g=g, cc=cc, k=k: E.matmul(
                                pt[:], lhsT=cd[:, t, k, cc * 128:(cc + 1) * 128], rhs=wf[:, g, k, :], start=(k == 0), stop=(k == 1)),
                                reads=[cdr, wfr], writes=[pr])
                        P.op("act", lambda E, pt=pt, t=t, g=g, cc=cc: E.activation(out=Ms[:, t, g, cc, :], in_=pt[:], func=AF.Copy),
                             reads=[pr], pw=[Msr])
            P.barrier()
            P.emit()
            P.release([cdr, wfr])
        for job in jobs:
            Skv, Sq, NB = job["Skv"], job["Sq"], job["NB"]
            NKT, NSB, NJ = Skv // 128, Sq // NB, NB // 128
            GB = min(NG, 4)
            with contextlib.ExitStack() as st:
                us, usr = sb(P, st, "us", [128, NKT, GB * 256], BF16)
                tabr = Ring(P, st, "tab", 2, [128, 2, NKT, NB], BF16)
                zfr = Ring(P, st, "zf", 2, [128, NJ, GB * 2, 128], BF16)
                ofr = Ring(P, st, "of", 2, [128, NJ, GB * 2, 128], BF16)
                abr = Ring(P, st, "abt", 2, [128, 2, 2, NB], BF16)
                pab = Ring(P, st, "pab", 4, [128, NB], F32, psum=True)
                pyr = Ring(P, st, "pyr", 2, [128, NB], F32, psum=True)
                ne = 0
                for gb in range(NG // GB):
                    P.dma("sp", us[:], job["u"][:, gb * GB * 256:(gb + 1) * GB * 256].rearrange("(kt p) c -> p kt c", p=128), writes=[usr])
                    for sbk in range(NSB):
                        tb, tbr = tabr.next()
                        P.newgen(tbr)
                        P.dma("sp", tb[:, 0], job["cst"][sbk], pw=[tbr])
                        P.dma("sp", tb[:, 1], job["sst"][sbk], pw=[tbr])
                        zc0 = NH + gb * GB * 2
                        zf, zfr_ = zfr.next()
                        P.dma("sp", zf[:], job["zsT"][sbk * NJ:(sbk + 1) * NJ, :, zc0:zc0 + GB * 2, :].rearrange("a p c t -> p a c t"), writes=[zfr_])
                        of, ofr_ = ofr.next()
                        P.newgen(ofr_)
                        for gl in range(GB):
                            g = gb * GB + gl
                            ab, abr_ = abr.next()
                            P.newgen(abr_)
                            for t in range(2):
                                for cc in range(2):
                                    pt, pr = pab.next()
                                    for kt in range(NKT):
                                        P.op("pe", lambda E, pt=pt, kt=kt, gl=gl, cc=cc, t=t, tb=tb: E.matmul(
                                            pt[:], lhsT=us[:, kt, gl * 256 + cc * 128: gl * 256 + (cc + 1) * 128], rhs=tb[:, t, kt, :],
                                            start=(kt == 0), stop=(kt == NKT - 1)), reads=[usr, tbr], writes=[pr])
                                    if ne % 2 == 0:
                                        P.op("act", lambda E, ab=ab, pt=pt, t=t, cc=cc: E.activation(out=ab[:, t, cc, :], in_=pt[:], func=AF.Copy),
                                             reads=[pr], pw=[abr_])
                                    else:
                                        P.op("dve", lambda E, ab=ab, pt=pt, t=t, cc=cc: E.tensor_copy(out=ab[:, t, cc, :], in_=pt[:]),
                                             reads=[pr], pw=[abr_])
                                    ne += 1
                            for dc in range(2):
                                py, pyr_ = pyr.next()
                                n = 0
                                for t in range(2):
                                    for cc in range(2):
                                        P.op("pe", lambda E, py=py, t=t, cc=cc, g=g, dc=dc, ab=ab, n=n: E.matmul(
                                            py[:], lhsT=Ms[:, t, g, cc, dc * 128:(dc + 1) * 128], rhs=ab[:, t, cc, :],
                                            start=(n == 0), stop=(n == 3)), reads=[Msr, abr_], writes=[pyr_])
                                        n += 1
                                ci = gl * 2 + dc
                                P.op("dve", lambda E, of=of, py=py, zf=zf, ci=ci: E.tensor_tensor(
                                    out=of[:, :, ci, :], in0=py[:].rearrange("p (a t) -> p a t", a=NJ), in1=zf[:, :, ci, :], op=ALU.mult),
                                    reads=[pyr_, zfr_], pw=[ofr_])
                        P.dma("sp", job["oT"][sbk * NJ:(sbk + 1) * NJ, :, zc0:zc0 + GB * 2, :].rearrange("a p c t -> p a c t"), of[:], reads=[ofr_])
                P.barrier()
                P.emit()
                for r in (tabr, zfr, ofr):
                    r.release()
                P.release([usr])


def phase4(P, nc, cfg, job, W, G):
    D, ZC = cfg["D"], cfg["ZC"]
    T4 = min(cfg["T4"], job["Sq"])
    NT = T4 // 128
    Sq, ji = job["Sq"], job["ji"]
    negh, neghr = G["negh"], G["neghr"]
    alpha = cfg["alpha"]
    NC = D // 256
    with contextlib.ExitStack() as st:
        lng, lngr = sb(P, st, "lng", [128, D], F32)
        lnb, lnbr = sb(P, st, "lnb", [128, D], F32)
        P.dma("sp", lng[:], W["ln_g"][0, :].partition_broadcast(128), writes=[lngr])
        P.dma("sp", lnb[:], W["ln_b"][0, :].partition_broadcast(128), writes=[lnbr])
        oT, oTr = sb(P, st, "oTs", [128, NT, ZC, 128], BF16)
        res, _ = sb(P, st, "res", [128, NT, D], F32)
        resr = [Res("res%d" % i) for i in range(NT)]
        wring = Ring(P, st, "wo", 3, [128, ZC, 256], BF16)
        gring = Ring(P, st, "gt", 3, [128, 2, 256], F32)
        tring = Ring(P, st, "tt", 3, [128, 512], F32)
        t2ring = Ring(P, st, "tt2", 3, [128, 512], F32)
        strg = Ring(P, st, "bst4", 2, [128, (D // 512) * 6], F32)
        smr = Ring(P, st, "sm4", 2, [128, 8], F32)
        pmm = Ring(P, st, "pm4", 4, [128, 512], F32, psum=True)
        for s_ in range(Sq // T4):
            P.dma("sp", oT[:], job["oT"][s_ * NT:(s_ + 1) * NT].rearrange("a p c t -> p a c t"), writes=[oTr])
            for tt in range(NT):
                r0 = s_ * T4 + tt * 128
                P.dma("sp", res[:, tt, :], job["x"][r0:r0 + 128, :], writes=[resr[tt]])
                P.newgen(resr[tt])
            wt_next = None
            for ct in range(NC):
                wt, wr = wring.next()
                P.dma("pool", wt[:], W["w_out_v"][:, :, ct * 256:(ct + 1) * 256], writes=[wr])
                gt, gr = gring.next()
                P.newgen(gr)
                P.dma("sp", gt[:, 0, :], W["modr"][ji, 2, ct * 256:(ct + 1) * 256].partition_broadcast(128), pw=[gr])
                P.dma("sp", gt[:, 1, :], W["modr"][ji, 3, ct * 256:(ct + 1) * 256].partition_broadcast(128), pw=[gr])
                for tp in range(NT // 2):
                    pt, pr = pmm.next()
                    for j in range(2):
                        tt = tp * 2 + j
                        for zc in range(ZC):
                            P.op("pe", lambda E, pt=pt, wt=wt, zc=zc, tt=tt, j=j: E.matmul(
                                pt[:, j * 256:(j + 1) * 256], lhsT=oT[:, tt, zc, :], rhs=wt[:, zc, :], start=(zc == 0), stop=(zc == ZC - 1)),
                                reads=[oTr, wr], writes=[pr])
                    t1, t1r = tring.next()
                    P.op("dve", lambda E, t1=t1, pt=pt, gt=gt: E.tensor_tensor(
                        out=t1[:].rearrange("p (j n) -> p j n", j=2), in0=pt[:].rearrange("p (j n) -> p j n", j=2),
                        in1=gt[:, 0:1, :].to_broadcast([128, 2, 256]), op=ALU.mult), reads=[pr, gr], writes=[t1r])
                    t2, t2r = t2ring.next()
                    P.op(EPOOL, lambda E, t2=t2, t1=t1, gt=gt: E.tensor_tensor(
                        out=t2[:].rearrange("p (j n) -> p j n", j=2), in0=t1[:].rearrange("p (j n) -> p j n", j=2),
                        in1=gt[:, 1:2, :].to_broadcast([128, 2, 256]), op=ALU.add), reads=[t1r, gr], writes=[t2r])
                    rs = res[:, tp * 2:tp * 2 + 2, ct * 256:(ct + 1) * 256]
                    P.op("dve", lambda E, rs=rs, t2=t2: E.scalar_tensor_tensor(
                        out=rs, in0=rs, scalar=alpha, in1=t2[:].rearrange("p (j n) -> p j n", j=2), op0=ALU.mult, op1=ALU.add),
                        reads=[t2r], pw=[resr[tp * 2], resr[tp * 2 + 1]])
            for tt in range(NT):
                r0 = s_ * T4 + tt * 128
                rt = res[:, tt, :]
                rr = resr[tt]
                bs, bsr = strg.next()
                P.newgen(bsr)
                for c in range(D // 512):
                    P.op("dve", lambda E, bs=bs, rt=rt, c=c: E.bn_stats(out=bs[:, c * 6:(c + 1) * 6], in_=rt[:, c * 512:(c + 1) * 512]),
                         reads=[rr], pw=[bsr])
                sm, smr_ = smr.next()
                P.op("dve", lambda E, sm=sm, bs=bs: E.bn_aggr(out=sm[:, 0:2], in_=bs[:]), reads=[bsr], writes=[smr_])
                P.op("dve", lambda E, sm=sm: E.tensor_scalar(out=sm[:, 2:3], in0=sm[:, 1:2], scalar1=LN_EPS, scalar2=None, op0=ALU.add),
                     reads=[smr_], writes=[smr_])
                P.op("act", lambda E, sm=sm: E.activation(out=sm[:, 5:6], in_=sm[:, 2:3], func=AF.Sqrt), reads=[smr_], writes=[smr_])
                P.op("dve", lambda E, sm=sm: E.reciprocal(out=sm[:, 3:4], in_=sm[:, 5:6]), reads=[smr_], writes=[smr_])
                P.op("dve", lambda E, sm=sm: E.tensor_scalar(out=sm[:, 4:5], in0=sm[:, 0:1], scalar1=sm[:, 3:4], scalar2=-1.0,
                                                            op0=ALU.mult, op1=ALU.mult), reads=[smr_], writes=[smr_])
                P.op("act", lambda E, rt=rt, sm=sm: E.activation(out=rt, in_=rt, func=AF.Identity, scale=sm[:, 3:4], bias=sm[:, 4:5]),
                     reads=[rr, smr_], writes=[rr])
                P.op(EPOOL, lambda E, rt=rt: E.tensor_tensor(out=rt, in0=rt, in1=lng[:], op=ALU.mult), reads=[rr, lngr], writes=[rr])
                P.op("dve", lambda E, rt=rt: E.tensor_tensor(out=rt, in0=rt, in1=lnb[:], op=ALU.add), reads=[rr, lnbr], writes=[rr])
                P.dma("sp", job["y"][r0:r0 + 128, :], rt, reads=[rr])
        P.barrier()
        P.emit()
        for r in (wring, gring):
            r.release()
        P.release([lngr, lnbr, oTr] + resr)


def _rope_tables(pos):
    inv = (ROPE_THETA ** (-np.arange(0, 64, 2, dtype=np.float32) / np.float32(64))).astype(np.float32)
    row = (pos // GRID_W).astype(np.float32)
    col = (pos % GRID_W).astype(np.float32)
    ar = row[:, None] * inv[None, :]
    ac = col[:, None] * inv[None, :]
    cosr, sinr, cosc, sinc = np.cos(ar), np.sin(ar), np.cos(ac), np.sin(ac)
    cosF = np.concatenate([cosr, cosr, cosc, cosc], axis=1).astype(np.float32)
    sinS = np.concatenate([-sinr, sinr, -sinc, sinc], axis=1).astype(np.float32)
    return np.tile(cosF, (1, 2)), np.tile(sinS, (1, 2))


def _dft_tables(S, kv_pos, own_pos, NB):
    prod = (kv_pos.astype(np.int64)[:, None] * own_pos.astype(np.int64)[None, :]) % S
    ang = prod.astype(np.float64) * (2.0 * np.pi / S)
    nrm = 1.0 / math.sqrt(S)
    out = []
    for f in (np.cos, np.sin):
        m = (f(ang) * nrm).astype(np.float32).astype(NPBF)
        Skv, Sq = m.shape
        m = m.reshape(Skv // 128, 128, Sq // NB, NB).transpose(2, 1, 0, 3)
        out.append(np.ascontiguousarray(m))
    return out


def _swap_gain(g):
    g = np.asarray(g, np.float32).reshape(2, 2, 32)
    return np.ascontiguousarray(g[:, ::-1, :]).reshape(128)


def host_prep(cfg, inputs, core):
    D, KC, SS, SP, SQP = cfg["D"], cfg["KC"], cfg["SS"], cfg["SP"], cfg["SQP"]
    f32 = np.float32
    m = {}
    m["w_ada"] = np.ascontiguousarray(inputs["w_ada"][0], f32)
    m["b_ada"] = np.ascontiguousarray(inputs["b_ada"][0:1], f32)
    m["w_in"] = np.ascontiguousarray(inputs["w_in"][0], f32)
    m["w_out"] = np.ascontiguousarray(inputs["w_out"][0], f32)
    m["w_four"] = np.ascontiguousarray(inputs["w_four"][0], f32)
    m["b_out"] = np.ascontiguousarray(inputs["b_out"][0:1], f32)
    m["ln_g"] = np.ascontiguousarray(inputs["ln_g"][0:1], f32)
    m["ln_b"] = np.ascontiguousarray(inputs["ln_b"][0:1], f32)
    qg = np.asarray(inputs["q_gain"][0], f32)
    kg = np.asarray(inputs["k_gain"][0], f32)
    m["gains"] = np.stack([np.tile(qg, 2), np.tile(_swap_gain(qg), 2), np.tile(kg, 2), np.tile(_swap_gain(kg), 2)]).astype(f32)
    c = np.arange(256)
    ang = ((c[:, None] * c[None, :]) % 256).astype(np.float64) * (2 * np.pi / 256)
    m["cdft"] = np.stack([np.cos(ang) / 16.0, -np.sin(ang) / 16.0]).astype(f32)
    m["xs"] = np.ascontiguousarray(inputs["x_sample"][core], f32)
    m["cs"] = np.ascontiguousarray(np.asarray(inputs["c_sample"][core], f32).reshape(KC, 128).T)
    pos = np.arange(SS)
    m["cos2s"], m["sin2s"] = _rope_tables(pos)
    m["csts"], m["ssts"] = _dft_tables(SS, pos, pos, cfg["NBS"])
    b, q = core // 4, core % 4
    own = np.arange(q * SQP, (q + 1) * SQP)
    rest = np.concatenate([np.arange(0, q * SQP), np.arange((q + 1) * SQP, SP)])
    order = np.concatenate([own, rest]).astype(np.int64)
    m["xp"] = np.ascontiguousarray(np.asarray(inputs["x_prompt"][b], f32)[order])
    m["cp"] = np.ascontiguousarray(np.asarray(inputs["c_prompt"][b], f32).reshape(KC, 128).T)
    m["cos2p"], m["sin2p"] = _rope_tables(order)
    m["cstp"], m["sstp"] = _dft_tables(SP, order, own, cfg["NBP"])
    return m


_CACHE = {}


def kernel(**inputs):
    cfg = make_cfg()
    if "nc" not in _CACHE:
        _CACHE["nc"] = build(cfg)
    nc = _CACHE["nc"]
    n = 8
    shared = None
    in_maps = []
    for core in range(n):
        m = host_prep(cfg, inputs, core)
        if shared is None:
            shared = m
        else:
            for k in ("w_ada", "b_ada", "w_in", "w_out", "w_four", "b_out", "ln_g", "ln_b", "gains", "cdft",
                      "cos2s", "sin2s", "csts", "ssts"):
                m[k] = shared[k]
        in_maps.append(m)
    res = run_bass_kernel_spmd(nc, in_maps, core_ids=list(range(n)))
    SQP = cfg["SQP"]
    y_s = np.stack([np.asarray(res.results[i]["ys"], np.float32) for i in range(n)], axis=0)
    y_p = np.zeros((2, cfg["SP"], cfg["D"]), np.float32)
    for i in range(n):
        b, q = i // 4, i % 4
        y_p[b, q * SQP:(q + 1) * SQP] = np.asarray(res.results[i]["yp"], np.float32)
    return (y_p, y_s)
```
